# Optimizing a Trainium2 kernel written in Bass

```python
import numpy as np
import jax
import jax.numpy as jnp
from jax import lax

D_MODEL = 1024
BATCH = 1
SEQ = 16384
DEPTH = 4

HEAD_DIM = 64
N_MIXERS = 2
A_Q_HEADS = 16
A_KV_HEADS = 4
A_HALF_WINDOW = 128
B_GROUPS = ((128, 1), (512, 4), (2048, 16))
B_Q_HEADS = 8
B_KV_HEADS = 2
D_FF = -(-8 * D_MODEL // (3 * 256)) * 256
A_QKV = (A_Q_HEADS + 2 * A_KV_HEADS) * HEAD_DIM
B_QKV = len(B_GROUPS) * (B_Q_HEADS + 2 * B_KV_HEADS) * HEAD_DIM
A_OUT = A_Q_HEADS * HEAD_DIM
B_OUT = B_Q_HEADS * HEAD_DIM
N_A_LAYERS = (DEPTH + 1) // 2
N_B_LAYERS = DEPTH // 2
RMS_EPS = 1e-6
NEG = -1e30

kernel_name = "hybrid_window_dilated_alibi_encoder"


def rmsnorm(x, g):
    xf = x.astype(jnp.float32)
    y = xf * lax.rsqrt(jnp.mean(xf * xf, axis=-1, keepdims=True) + RMS_EPS)
    return (y * g.astype(jnp.float32)).astype(x.dtype)


def alibi_slopes(n):
    return jnp.asarray(2.0 ** (-8.0 * np.arange(1, n + 1) / n), dtype=jnp.float32)


def banded_attention(q, k, v, half_window, slopes, dist_scale, sink=None):
    n, L, hkv, g, dh = q.shape
    blk = half_window
    nb = -(-L // blk)
    lp = nb * blk
    qb = jnp.pad(q, ((0, 0), (0, lp - L), (0, 0), (0, 0), (0, 0))).reshape(n, nb, blk, hkv, g, dh)
    pad_kv = ((0, 0), (blk, lp - L + blk), (0, 0), (0, 0))
    kb = jnp.pad(k, pad_kv).reshape(n, nb + 2, blk, hkv, dh)
    vb = jnp.pad(v, pad_kv).reshape(n, nb + 2, blk, hkv, dh)
    kw = jnp.concatenate([kb[:, :-2], kb[:, 1:-1], kb[:, 2:]], axis=2)
    vw = jnp.concatenate([vb[:, :-2], vb[:, 1:-1], vb[:, 2:]], axis=2)
    rel = jnp.arange(3 * blk)[None, :] - blk - jnp.arange(blk)[:, None]
    kpos = jnp.arange(nb)[:, None] * blk - blk + jnp.arange(3 * blk)[None, :]
    mask = (jnp.abs(rel) <= half_window)[None] & ((kpos >= 0) & (kpos < L))[:, None, :]
    dist = (dist_scale * jnp.abs(rel)).astype(jnp.float32)
    bias = -slopes.astype(jnp.float32)[:, :, None, None] * dist
    s = jnp.einsum('nbqhgd,nbkhd->nbhgqk', qb, kw,
                   preferred_element_type=jnp.float32) * (dh ** -0.5) + bias
    s = jnp.where(mask[None, :, None, None], s, NEG)
    m = s.max(axis=-1)
    if sink is not None:
        sk = sink.astype(jnp.float32)[:, :, None]
        m = jnp.maximum(m, sk)
    p = jnp.exp(s - m[..., None])
    l = p.sum(axis=-1)
    if sink is not None:
        l = l + jnp.exp(sk - m)
    o = jnp.einsum('nbhgqk,nbkhd->nbqhgd', p, vw.astype(jnp.float32))
    o = o / jnp.moveaxis(l, -1, 2)[..., None]
    lse = jnp.moveaxis(m + jnp.log(l), -1, 2)
    o = o.reshape(n, lp, hkv, g, dh)[:, :L].astype(q.dtype)
    lse = lse.reshape(n, lp, hkv, g)[:, :L]
    return o, lse


def to_strided(t, dil):
    b, s = t.shape[:2]
    rest = t.shape[2:]
    t = t.reshape(b, s // dil, dil, *rest)
    return jnp.moveaxis(t, 2, 1).reshape(b * dil, s // dil, *rest)


def from_strided(t, b, dil):
    l = t.shape[1]
    rest = t.shape[2:]
    t = t.reshape(b, dil, l, *rest)
    return jnp.moveaxis(t, 1, 2).reshape(b, l * dil, *rest)


def windowed_gqa(h, w_in, w_out, sink):
    b, s, _ = h.shape
    gq = A_Q_HEADS // A_KV_HEADS
    dq, dk = A_Q_HEADS * HEAD_DIM, A_KV_HEADS * HEAD_DIM
    qkv = h @ w_in
    q = qkv[..., :dq].reshape(b, s, A_KV_HEADS, gq, HEAD_DIM)
    k = qkv[..., dq:dq + dk].reshape(b, s, A_KV_HEADS, HEAD_DIM)
    v = qkv[..., dq + dk:].reshape(b, s, A_KV_HEADS, HEAD_DIM)
    slopes = alibi_slopes(A_Q_HEADS).reshape(A_KV_HEADS, gq)
    o, _ = banded_attention(q, k, v, A_HALF_WINDOW, slopes, 1, sink.reshape(A_KV_HEADS, gq))
    return o.reshape(b, s, A_OUT) @ w_out


def dilated_attention(h, w_in, w_out):
    b, s, _ = h.shape
    ng = len(B_GROUPS)
    gq = B_Q_HEADS // B_KV_HEADS
    dq, dk = ng * B_Q_HEADS * HEAD_DIM, ng * B_KV_HEADS * HEAD_DIM
    qkv = h @ w_in
    q = qkv[..., :dq].reshape(b, s, ng, B_KV_HEADS, gq, HEAD_DIM)
    k = qkv[..., dq:dq + dk].reshape(b, s, ng, B_KV_HEADS, HEAD_DIM)
    v = qkv[..., dq + dk:].reshape(b, s, ng, B_KV_HEADS, HEAD_DIM)
    slopes = alibi_slopes(ng * B_Q_HEADS).reshape(ng, B_KV_HEADS, gq)
    outs, lses = [], []
    for gi, (window, dil) in enumerate(B_GROUPS):
        half = window // (2 * dil)
        o, lse = banded_attention(to_strided(q[:, :, gi], dil), to_strided(k[:, :, gi], dil),
                                  to_strided(v[:, :, gi], dil), half, slopes[gi], dil)
        outs.append(from_strided(o, b, dil))
        lses.append(from_strided(lse, b, dil))
    alpha = jax.nn.softmax(jnp.stack(lses), axis=0)
    o = jnp.einsum('nbshg,nbshgd->bshgd', alpha,
                   jnp.stack(outs).astype(jnp.float32)).astype(h.dtype)
    return o.reshape(b, s, B_OUT) @ w_out


def swiglu(h, w_in, w_out):
    gu = h @ w_in
    gate, up = jnp.split(gu, 2, axis=-1)
    return (jax.nn.silu(gate) * up) @ w_out


def setup_inputs(seed: int = 0) -> dict:
    key = jax.random.key(seed)
    ks = jax.random.split(key, 16)
    D = D_MODEL
    nrm = lambda k, shape, scale: jax.random.normal(k, shape, jnp.float32) * scale
    return {
        "x": nrm(ks[0], (BATCH, SEQ, D), 1.0),
        "c": nrm(ks[1], (BATCH, D), 1.0),
        "ada_w": nrm(ks[2], (DEPTH, D, 6 * D), 0.5 * D ** -0.5),
        "ada_b": nrm(ks[3], (DEPTH, 6 * D), 0.1),
        "norm_mix": 1.0 + nrm(ks[4], (DEPTH, D), 0.05),
        "norm_ffn": 1.0 + nrm(ks[5], (DEPTH, D), 0.05),
        "ffn_w_in": nrm(ks[6], (DEPTH, D, 2 * D_FF), D ** -0.5),
        "ffn_w_out": nrm(ks[7], (DEPTH, D_FF, D), D_FF ** -0.5),
        "a_w_in": nrm(ks[8], (N_A_LAYERS, D, A_QKV), D ** -0.5),
        "a_w_out": nrm(ks[9], (N_A_LAYERS, A_OUT, D), A_OUT ** -0.5),
        "a_sink": nrm(ks[10], (N_A_LAYERS, A_Q_HEADS), 0.5),
        "b_w_in": nrm(ks[11], (N_B_LAYERS, D, B_QKV), D ** -0.5),
        "b_w_out": nrm(ks[12], (N_B_LAYERS, B_OUT, D), B_OUT ** -0.5),
        "final_norm": 1.0 + nrm(ks[13], (D,), 0.05),
    }


def reference(x, c, ada_w, ada_b, norm_mix, norm_ffn, ffn_w_in, ffn_w_out,
              a_w_in, a_w_out, a_sink, b_w_in, b_w_out, final_norm):
    cond = jax.nn.silu(c)
    for i in range(DEPTH):
        mod = (cond @ ada_w[i] + ada_b[i])[:, None, :]
        sh1, sc1, g1, sh2, sc2, g2 = jnp.split(mod, 6, axis=-1)
        h = rmsnorm(x, norm_mix[i]) * (1 + sc1) + sh1
        j = i // N_MIXERS
        if i % N_MIXERS == 0:
            y = windowed_gqa(h, a_w_in[j], a_w_out[j], a_sink[j])
        else:
            y = dilated_attention(h, b_w_in[j], b_w_out[j])
        x = x + g1 * y
        h = rmsnorm(x, norm_ffn[i]) * (1 + sc2) + sh2
        x = x + g2 * swiglu(h, ffn_w_in[i], ffn_w_out[i])
    return rmsnorm(x, final_norm)
```

```python
import contextlib
import numpy as np
import ml_dtypes
import concourse.bass as bass
import concourse.mybir as mybir
from concourse.bass_utils import run_bass_kernel_spmd

F32 = mybir.dt.float32
BF16 = mybir.dt.bfloat16
AF = mybir.ActivationFunctionType
ALU = mybir.AluOpType
NPBF = ml_dtypes.bfloat16

NCORE = 8
NT = 2048
SEQ = 16384
D = 1024
KC = 8
TB = 512
NB = 4
DFF = 2816
NFF = 22
EPS = 1e-6
MASKV = -1.0e4
FF_GROUPS = [(0, 4), (4, 4), (8, 4), (12, 4), (16, 3), (19, 3)]
B_DIL = (1, 4, 16)
ARENA_WORDS = 52000

ENGS = ("pe", "act", "dve", "pool", "sp")


class Tok:
    __slots__ = ("name", "writer", "readers")

    def __init__(self, name):
        self.name = name
        self.writer = None
        self.readers = []


class Prog:
    def __init__(self, nc, n_dma_sems=24):
        self.nc = nc
        self.q = {e: [] for e in ENGS}
        self.cnt = {e: 0 for e in ENGS}
        self.waited = {e: {} for e in ENGS}
        self.n_dma_sems = n_dma_sems
        self.dma_cnt = [0] * n_dma_sems
        self.dma_rr = 0
        self.sems = {}

    def _need(self, eng, deps):
        w = self.waited[eng]
        out = {}
        for d in deps:
            if d is None:
                continue
            k, v = d
            if w.get(k, 0) >= v:
                continue
            if out.get(k, 0) < v:
                out[k] = v
        for k, v in out.items():
            w[k] = v
        return list(out.items())

    def _deps(self, reads, writes):
        deps = []
        for t in reads:
            deps.append(t.writer)
        for t in writes:
            deps.append(t.writer)
            deps.extend(t.readers)
        return deps

    def _mark(self, me, reads, writes):
        for t in reads:
            t.readers.append(me)
        for t in writes:
            t.writer = me
            t.readers = []

    def op(self, eng, fn, reads=(), writes=()):
        deps = self._deps(reads, writes)
        if eng == "pe":
            deps = [d for d in deps if d is not None and d[0] != "pe"]
        waits = self._need(eng, deps)
        self.cnt[eng] += 1
        me = (eng, self.cnt[eng])
        self.q[eng].append((waits, fn, (eng, 1)))
        self._mark(me, reads, writes)
        return me

    def dma(self, eng, fn, reads=(), writes=()):
        deps = self._deps(reads, writes)
        k = self.dma_rr
        self.dma_rr = (self.dma_rr + 1) % self.n_dma_sems
        key = ("dma", k)
        if self.dma_cnt[k] > 0:
            deps.append((key, self.dma_cnt[k]))
        waits = self._need(eng, deps)
        self.dma_cnt[k] += 16
        me = (key, self.dma_cnt[k])
        self.q[eng].append((waits, fn, (key, 16)))
        self._mark(me, reads, writes)
        return me

    def barrier(self):
        for e in ENGS:
            deps = [(x, self.cnt[x]) for x in ENGS if x != e and self.cnt[x] > 0]
            deps += [(("dma", k), self.dma_cnt[k]) for k in range(self.n_dma_sems) if self.dma_cnt[k] > 0]
            waits = self._need(e, deps)
            if waits:
                self.q[e].append((waits, None, None))

    def final_wait(self, eng, toks):
        deps = [t.writer for t in toks]
        waits = self._need(eng, deps)
        self.q[eng].append((waits, None, None))

    def emit(self):
        nc = self.nc
        with contextlib.ExitStack() as st:
            for e in ENGS:
                self.sems[e] = st.enter_context(nc.semaphore(f"s_{e}"))
            for k in range(self.n_dma_sems):
                self.sems[("dma", k)] = st.enter_context(nc.semaphore(f"s_dma{k}"))
            block = st.enter_context(nc.Block())

            def run(engname):
                def body(engine):
                    for waits, fn, inc in self.q[engname]:
                        for k, v in waits:
                            engine.wait_ge(self.sems[k], v)
                        if fn is not None:
                            ins = fn(engine)
                            ins.then_inc(self.sems[inc[0]], inc[1])
                return body

            block.tensor(run("pe"))
            block.scalar(run("act"))
            block.vector(run("dve"))
            block.gpsimd(run("pool"))
            block.sync(run("sp"))


class Mem:
    def __init__(self, nc, words):
        self.arena = nc.alloc_sbuf_tensor("arena", [128, words], F32)
        self.words = words
        self.top = 0

    def mark(self):
        return self.top

    def release(self, m):
        self.top = m

    def f32(self, n):
        off = self.top
        self.top += (n + 7) // 8 * 8
        assert self.top <= self.words, f"SBUF arena overflow {self.top} > {self.words}"
        return self.arena[:, off:off + n]

    def bf16(self, n):
        w = (n + 1) // 2
        off = self.top
        self.top += (w + 7) // 8 * 8
        assert self.top <= self.words, f"SBUF arena overflow {self.top} > {self.words}"
        return self.arena[:, off:off + w].bitcast(BF16)[:, 0:n]


class Ctx:
    def __init__(self, nc):
        self.nc = nc
        self.P = Prog(nc)
        self.mem = Mem(nc, ARENA_WORDS)
        self.ps = [nc.alloc_psum_tensor(f"psb{b}", [128, 512], F32) for b in range(8)]
        self.pst = [Tok(f"ps{b}") for b in range(8)]
        self.dram = {}
        self.out_toks = []

    def din(self, name, shape, dt=F32):
        t = self.nc.dram_tensor(name, list(shape), dt, kind="ExternalInput").ap()
        self.dram[name] = t
        return t

    def dout(self, name, shape, dt=F32):
        t = self.nc.dram_tensor(name, list(shape), dt, kind="ExternalOutput").ap()
        self.dram[name] = t
        return t

    def mm(self, out, lhsT, rhs, start, stop, reads, writes):
        self.P.op("pe", lambda e: e.matmul(out, lhsT=lhsT, rhs=rhs, start=start, stop=stop), reads, writes)

    def act(self, out, in_, func, reads, writes, bias=None, scale=None):
        kw = {}
        if bias is not None:
            kw["bias"] = bias
        if scale is not None:
            kw["scale"] = scale
        self.P.op("act", lambda e: e.activation(out=out, in_=in_, func=func, **kw), reads, writes)

    def tt(self, eng, out, in0, in1, op, reads, writes):
        self.P.op(eng, lambda e: e.tensor_tensor(out=out, in0=in0, in1=in1, op=op), reads, writes)

    def ts(self, eng, out, in0, s1, s2, op0, op1, reads, writes):
        if op1 is None:
            self.P.op(eng, lambda e: e.tensor_scalar(out=out, in0=in0, scalar1=s1, scalar2=None, op0=op0), reads, writes)
        else:
            self.P.op(eng, lambda e: e.tensor_scalar(out=out, in0=in0, scalar1=s1, scalar2=s2, op0=op0, op1=op1),
                      reads, writes)

    def stt(self, out, in0, scalar, in1, op0, op1, reads, writes):
        self.P.op("dve", lambda e: e.scalar_tensor_tensor(out=out, in0=in0, scalar=scalar, in1=in1, op0=op0, op1=op1),
                  reads, writes)

    def copy(self, eng, out, in_, reads, writes):
        if eng == "act":
            self.P.op("act", lambda e: e.activation(out=out, in_=in_, func=AF.Copy), reads, writes)
        else:
            self.P.op(eng, lambda e: e.tensor_copy(out=out, in_=in_), reads, writes)

    def memset(self, eng, ap, val, writes):
        self.P.op(eng, lambda e: e.memset(ap, val), (), writes)

    def dma(self, q, out, in_, reads, writes):
        self.P.dma(q, lambda e: e.dma_start(out=out, in_=in_), reads, writes)


def v3(ap, a):
    return ap.rearrange("p (a b) -> p a b", a=a)


def v4(ap, a, b):
    return ap.rearrange("p (a b c) -> p a b c", a=a, b=b)


def setup_common(C, need_x=True):
    m = C.mem
    if need_x:
        C.xT = v3(m.f32(KC * NT), KC)
        C.tx = [[Tok(f"x{kc}_{b}") for b in range(NB)] for kc in range(KC)]
        xd = C.din("xT", [D, NT]).rearrange("(kc p) t -> p kc t", p=128)
        for kc in range(KC):
            for b in range(NB):
                C.dma("sp", C.xT[:, kc, b * TB:(b + 1) * TB], xd[:, kc, b * TB:(b + 1) * TB], [], [C.tx[kc][b]])
    C.mod = m.f32(192)
    C.gain = m.f32(72)
    C.emask = m.f32(4)
    C.gsc = m.f32(64)
    C.gfin = m.f32(8)
    C.ones = m.bf16(128)
    C.t_const = Tok("const")
    tmp = m.f32(8)
    C.dma("sp", C.mod, C.din("modT", [128, 192]), [], [C.t_const])
    C.dma("sp", C.gain, C.din("gainT", [128, 72]), [], [C.t_const])
    C.dma("sp", C.emask, C.din("emask", [128, 4]), [], [C.t_const])
    C.memset("pool", C.ones, 1.0, [C.t_const])
    C.epsb = m.f32(1)
    C.memset("pool", C.epsb, 1024.0 * EPS, [C.t_const])
    for i in range(4):
        for w in range(2):
            sc = C.mod[:, i * 48 + (3 * w + 1) * 8: i * 48 + (3 * w + 1) * 8 + 8]
            g = C.gain[:, (w * 4 + i) * 8:(w * 4 + i) * 8 + 8]
            o = C.gsc[:, (i * 2 + w) * 8:(i * 2 + w) * 8 + 8]
            C.ts("dve", tmp, sc, 1.0, 32.0, ALU.add, ALU.mult, [C.t_const], [C.t_const])
            C.tt("dve", o, tmp, g, ALU.mult, [C.t_const], [C.t_const])
    C.ts("dve", C.gfin, C.gain[:, 64:72], 32.0, None, ALU.mult, None, [C.t_const], [C.t_const])


def mod_vec(C, i, v):
    return C.mod[:, i * 48 + v * 8: i * 48 + v * 8 + 8]


def emit_norm(C, gsc, sh, dst_fn, dst_tok_fn, after_blk=None):
    m = C.mem
    sq = [v3(m.bf16(KC * TB), KC) for _ in range(2)]
    tsq = [[Tok(f"sq{i}_{kc}") for kc in range(KC)] for i in range(2)]
    rstd = [m.f32(TB) for _ in range(2)]
    trs = [Tok("rstd0"), Tok("rstd1")]
    y = [m.f32(TB) for _ in range(2)]
    ty = [Tok("y0"), Tok("y1")]
    bank, tb = C.ps[7], C.pst[7]
    for b in range(NB):
        s = sq[b % 2]
        ts_ = tsq[b % 2]
        sl = slice(b * TB, (b + 1) * TB)
        for kc in range(KC):
            C.act(s[:, kc, :], C.xT[:, kc, sl], AF.Square, [C.tx[kc][b]], [ts_[kc]])
        for kc in range(KC):
            C.mm(bank[:, :], C.ones, s[:, kc, :], kc == 0, kc == KC - 1, [ts_[kc], C.t_const], [tb])
        r = rstd[b % 2]
        C.act(r, bank[:, :], AF.Ln, [tb, C.t_const], [trs[b % 2]], bias=C.epsb[:, 0:1])
        C.act(r, r, AF.Exp, [trs[b % 2]], [trs[b % 2]], scale=-0.5)
        for kc in range(KC):
            yy = y[kc % 2]
            C.tt("dve", yy, C.xT[:, kc, sl], r, ALU.mult, [C.tx[kc][b], trs[b % 2]], [ty[kc % 2]])
            if sh is not None:
                C.act(dst_fn(kc, b), yy, AF.Identity, [ty[kc % 2], C.t_const], [dst_tok_fn(kc, b)],
                      bias=sh[:, kc:kc + 1], scale=gsc[:, kc:kc + 1])
            else:
                C.act(dst_fn(kc, b), yy, AF.Identity, [ty[kc % 2], C.t_const], [dst_tok_fn(kc, b)],
                      scale=gsc[:, kc:kc + 1])
            if after_blk is not None:
                after_blk(kc, b)


def load_w(C, dst, src, tok):
    C.dma("pool", dst, src, [], [tok])


def emit_h(C, layer, which):
    hT = v3(C.mem.bf16(KC * NT), KC)
    th = [[Tok(f"h{kc}_{b}") for b in range(NB)] for kc in range(KC)]
    gsc = C.gsc[:, (layer * 2 + which) * 8:(layer * 2 + which) * 8 + 8]
    sh = mod_vec(C, layer, 3 * which)
    emit_norm(C, gsc, sh, lambda kc, b: hT[:, kc, b * TB:(b + 1) * TB], lambda kc, b: th[kc][b])
    return hT, th


def emit_kv(C, typ):
    m = C.mem
    mk = m.mark()
    CK = 256 if typ == "A" else 384
    nch, nkv = CK // 128, CK // 64
    layer = C.kv_layer
    wk = v3(m.bf16(KC * CK), KC)
    wv = v3(m.bf16(KC * CK), KC)
    twk, twv = Tok("wk"), Tok("wv")
    wkd = C.din("wk", [D, CK]).rearrange("(kc p) n -> p kc n", p=128)
    wvd = C.din("wv", [D, CK]).rearrange("(kc p) n -> p kc n", p=128)
    load_w(C, wk, wkd, twk)
    load_w(C, wv, wvd, twv)
    kT = v3(m.bf16(nch * NT), nch)
    tk = [Tok(f"kT{c}") for c in range(nch)]
    Vp = m.bf16(16 * nkv * 128)
    tv = Tok("Vp")
    C.memset("pool", Vp, 0.0, [tv])
    Vp5 = Vp.rearrange("p (t k two d) -> p t k two d", t=16, k=nkv // 2, two=2)
    hT, th = emit_h(C, layer, 0)
    KTd = C.dout("KT", [CK, NT], BF16)
    Vpd = C.dout("Vp", [NT, nkv, 128], BF16)
    cnt = 0
    for b in range(NB):
        sl = slice(b * TB, (b + 1) * TB)
        for ch in range(nch):
            bk = 5 + (cnt % 2)
            cnt += 1
            for kc in range(KC):
                C.mm(C.ps[bk][:, :], wk[:, kc, ch * 128:(ch + 1) * 128], hT[:, kc, sl], kc == 0, kc == KC - 1,
                     [twk, th[kc][b]], [C.pst[bk]])
            C.copy("act", kT[:, ch, sl], C.ps[bk][:, :], [C.pst[bk]], [tk[ch]])
    for ch in range(nch):
        C.dma("sp", KTd[ch * 128:(ch + 1) * 128, :], kT[:, ch, :], [tk[ch]], [tk[ch]])
    for tt_ in range(16):
        b = tt_ // 4
        bk = 5 + (cnt % 2)
        cnt += 1
        for kc in range(KC):
            C.mm(C.ps[bk][:, 0:CK], hT[:, kc, tt_ * 128:(tt_ + 1) * 128], wv[:, kc, :], kc == 0, kc == KC - 1,
                 [twv, th[kc][b]], [C.pst[bk]])
        pv = C.ps[bk][:, 0:CK].rearrange("p (k two d) -> p k two d", k=nkv // 2, two=2)
        C.copy("dve", Vp5[:, tt_, :, 0, 0:64], pv[:, :, 0, :], [C.pst[bk]], [tv])
        C.copy("act", Vp5[:, tt_, :, 1, 64:128], pv[:, :, 1, :], [C.pst[bk]], [tv])
    C.dma("sp", Vpd.rearrange("(t p) k d -> p t k d", p=128), Vp.rearrange("p (t k d) -> p t k d", t=16, k=nkv),
          [tv], [tv])
    C.out_toks += tk + [tv]
    C.P.barrier()
    m.release(mk)


def emit_q(C, layer, wq_name, ncol):
    m = C.mem
    nq = ncol // 128
    qT = v3(m.bf16(nq * NT), nq)
    tq = [[Tok(f"q{c}_{b}") for b in range(NB)] for c in range(nq)]
    mk = m.mark()
    wq = v3(m.bf16(KC * ncol), KC)
    twq = Tok("wq")
    wqd = C.din(wq_name, [D, ncol]).rearrange("(kc p) n -> p kc n", p=128)
    for kc in range(KC):
        load_w(C, wq[:, kc, :], wqd[:, kc, :], twq)
    hT, th = emit_h(C, layer, 0)
    cnt = 0
    for b in range(NB):
        sl = slice(b * TB, (b + 1) * TB)
        for c in range(nq):
            bk = 5 + (cnt % 2)
            cnt += 1
            for kc in range(KC):
                C.mm(C.ps[bk][:, :], wq[:, kc, c * 128:(c + 1) * 128], hT[:, kc, sl], kc == 0, kc == KC - 1,
                     [twq, th[kc][b]], [C.pst[bk]])
            C.act(qT[:, c, sl], C.ps[bk][:, :], AF.Copy, [C.pst[bk]], [tq[c][b]], scale=0.125)
    C.P.barrier()
    m.release(mk)
    return qT, tq


def emit_finalize_outproj(C, layer, b, R, nfc, accO_fn, taccO, accL_ap, taccL, sel, wo, two, tmp):
    rl, rh, rlo, rlbc, OTn, t_rl, t_rlbc, t_otn = tmp
    sl = slice(b * TB, (b + 1) * TB)
    C.P.op("dve", lambda e: e.reciprocal(out=rl[0:R, :], in_=accL_ap), [taccL], [t_rl])
    C.copy("dve", rh[0:R, :], rl[0:R, :], [t_rl], [t_rl])
    C.tt("dve", rlo[0:R, :], rl[0:R, :], rh[0:R, :], ALU.subtract, [t_rl], [t_rl])
    for fc in range(nfc):
        bk = 5 + (fc % 2)
        C.mm(C.ps[bk][:, :], sel[0:R, fc, :], rh[0:R, :], True, False, [t_rl, C.t_const], [C.pst[bk]])
        C.mm(C.ps[bk][:, :], sel[0:R, fc, :], rlo[0:R, :], False, True, [t_rl, C.t_const], [C.pst[bk]])
        C.copy("act", rlbc[fc % 2], C.ps[bk][:, :], [C.pst[bk]], [t_rlbc[fc % 2]])
        C.tt("dve", OTn[:, fc, :], accO_fn(fc), rlbc[fc % 2], ALU.mult, [taccO, t_rlbc[fc % 2]], [t_otn[fc]])
    g1 = mod_vec(C, layer, 2)
    for mch in range(KC):
        bk = 5 + (mch % 2)
        for fc in range(nfc):
            C.mm(C.ps[bk][:, :], wo[:, fc, mch * 128:(mch + 1) * 128], OTn[:, fc, :], fc == 0, fc == nfc - 1,
                 [two, t_otn[fc]], [C.pst[bk]])
        xs = C.xT[:, mch, sl]
        C.stt(xs, C.ps[bk][:, :], g1[:, mch:mch + 1], xs, ALU.mult, ALU.add,
              [C.pst[bk], C.tx[mch][b], C.t_const], [C.tx[mch][b]])


def alloc_fin_tmp(C, nfc):
    m = C.mem
    rl = m.f32(TB)
    rh = m.bf16(TB)
    rlo = m.bf16(TB)
    rlbc = [m.f32(TB), m.f32(TB)]
    OTn = v3(m.bf16(nfc * TB), nfc)
    return (rl, rh, rlo, rlbc, OTn, Tok("rl"), [Tok("rlbc0"), Tok("rlbc1")], [Tok(f"otn{f}") for f in range(nfc)])


def emit_attn_A(C, layer):
    m = C.mem
    P = C.P
    mk0 = m.mark()
    qT, tq = emit_q(C, layer, "wq", 1024)
    bmat = v4(m.bf16(4 * 2 * 512), 4, 2)
    dmat = v3(m.bf16(3 * 128), 3)
    oneh = v3(m.bf16(16 * 16), 16)
    sel = v3(m.bf16(8 * 128), 8)
    esink = m.f32(1)
    tc = Tok("attc")
    C.dma("sp", bmat, C.din("bmat", [128, 4, 2, 512], BF16), [], [tc])
    C.dma("sp", dmat, C.din("dmat", [128, 3, 128], BF16), [], [tc])
    C.dma("sp", oneh, C.din("oneh", [128, 16, 16], BF16), [], [tc])
    C.dma("sp", sel[0:16], C.din("sel", [16, 8, 128], BF16), [], [tc])
    C.dma("sp", esink[0:16, :], C.din("sink", [16, 1]), [], [tc])
    C.act(esink[0:16, :], esink[0:16, :], AF.Exp, [tc], [tc])
    wo = v3(m.bf16(KC * D), KC)
    two = Tok("wo")
    wod = C.din("wo", [D, D]).rearrange("(kc p) n -> p kc n", p=128)
    for kc in range(KC):
        load_w(C, wo[:, kc, :], wod[:, kc, :], two)
    kTw = [v3(m.bf16(2 * 768), 2) for _ in range(2)]
    Vw = [v4(m.bf16(6 * 4 * 128), 6, 4) for _ in range(2)]
    tkw = [Tok("kTw0"), Tok("kTw1")]
    tvw = [Tok("Vw0"), Tok("Vw1")]
    accO = v3(m.f32(8 * TB), 8)
    taccO = Tok("accO")
    accL = m.f32(TB)
    taccL = Tok("accL")
    PT = [m.bf16(512), m.bf16(512)]
    tpt = [Tok("pt0"), Tok("pt1")]
    fin = alloc_fin_tmp(C, 8)
    KTd = C.din("KTw", [256, NT + 256], BF16)
    Vd = C.din("Vw", [NT + 256, 4, 128], BF16).rearrange("(t p) k d -> p t k d", p=128)
    scnt = 0
    for b in range(NB):
        kb, vb = kTw[b % 2], Vw[b % 2]
        for ch in range(2):
            C.dma("sp", kb[:, ch, :], KTd[ch * 128:(ch + 1) * 128, b * TB:b * TB + 768], [], [tkw[b % 2]])
        C.dma("sp", vb, Vd[:, 4 * b:4 * b + 6], [], [tvw[b % 2]])
        for qi in range(4):
            j = 4 * b + qi
            qs = slice(b * TB + qi * 128, b * TB + (qi + 1) * 128)
            lbank, tl = C.ps[4], C.pst[4]
            first_l = True
            for pair in range(2):
                obank, to = C.ps[2 + pair], C.pst[2 + pair]
                first_o = True
                for g in (2 * pair, 2 * pair + 1):
                    half = g % 2
                    hs = slice(half * 64, half * 64 + 64)
                    for c in range(3):
                        sb = scnt % 2
                        scnt += 1
                        sbank, tsb = C.ps[sb], C.pst[sb]
                        C.mm(sbank[:, :], dmat[:, c, :], bmat[:, g, 0, :], True, False, [tc], [tsb])
                        C.mm(sbank[:, :], dmat[:, c, :], bmat[:, g, 1, :], False, False, [tc], [tsb])
                        C.mm(sbank[:, :], kb[hs, pair, (qi + c) * 128:(qi + c + 1) * 128],
                             qT[hs, 4 * pair:4 * pair + 4, qs], False, True,
                             [tkw[b % 2]] + [tq[4 * pair + i][b] for i in range(4)], [tsb])
                        bias = None
                        if j == 0 and c == 0:
                            bias = C.emask[:, 0:1]
                        if j == 15 and c == 2:
                            bias = C.emask[:, 1:2]
                        pt, tp = PT[sb], tpt[sb]
                        C.act(pt, sbank[:, :], AF.Exp, [tsb, C.t_const], [tp], bias=bias)
                        last_o = (g == 2 * pair + 1 and c == 2)
                        C.mm(obank[:, :], vb[:, qi + c, g, :], pt, first_o, last_o, [tp, tvw[b % 2]], [to])
                        first_o = False
                        for i in range(4):
                            last_l = (pair == 1 and last_o and i == 3)
                            C.mm(lbank[0:16, 0:128], oneh[:, 4 * g + i, :], pt[:, i * 128:(i + 1) * 128],
                                 first_l, last_l, [tp, tc], [tl])
                            first_l = False
                C.copy("act", accO[:, 4 * pair:4 * pair + 4, qi * 128:(qi + 1) * 128],
                       obank[:, :].rearrange("p (i q) -> p i q", i=4), [to], [taccO])
            C.ts("dve", accL[0:16, qi * 128:(qi + 1) * 128], lbank[0:16, 0:128], esink[0:16, 0:1], None,
                 ALU.add, None, [tl, tc], [taccL])
        emit_finalize_outproj(C, layer, b, 16, 8, lambda fc: accO[:, fc, :], taccO, accL[0:16, :], taccL,
                              sel, wo, two, fin)
    P.barrier()
    m.release(mk0)


def emit_attn_B(C, layer):
    m = C.mem
    P = C.P
    mk0 = m.mark()
    qT, tq = emit_q(C, layer, "wq", 1536)
    accO = v3(m.f32(4 * NT), 4)
    accL = m.f32(NT)
    taccO, taccL = Tok("accO"), Tok("accL")
    C.memset("pool", accO.rearrange("p a b -> p (a b)"), 0.0, [taccO])
    C.memset("pool", accL, 0.0, [taccL])
    oneh = v3(m.bf16(8 * 8), 8)
    sel = v3(m.bf16(4 * 128), 4)
    tc = Tok("attc")
    C.dma("sp", oneh, C.din("oneh", [128, 8, 8], BF16), [], [tc])
    C.dma("sp", sel[0:8], C.din("sel", [8, 4, 128], BF16), [], [tc])
    mk1 = m.mark()
    bmat = v4(m.bf16(6 * 2 * 512), 6, 2)
    dmat = v4(m.bf16(3 * 2 * 128), 3, 2)
    C.dma("sp", bmat, C.din("bmat", [128, 6, 2, 512], BF16), [], [tc])
    C.dma("sp", dmat, C.din("dmat", [128, 3, 2, 128], BF16), [], [tc])
    PT = [m.bf16(512), m.bf16(512)]
    tpt = [Tok("pt0"), Tok("pt1")]
    WMAX = NT + 128 * 16
    kTw_buf = m.bf16(WMAX)
    Vw_buf = m.bf16(32 * 2 * 128)
    tkw, tvw = Tok("kTw"), Tok("Vw")
    scnt = 0
    ocnt = 0
    for gi, d in enumerate(B_DIL):
        W = NT + 128 * d
        ncw = 16 // d + 1
        U = NT // d
        nut = U // 128
        kTw = kTw_buf[:, 0:W]
        Vw = v4(Vw_buf[:, 0:d * ncw * 256], d * ncw, 2)
        C.dma("sp", kTw, C.din(f"KTw{gi}", [128, W], BF16), [], [tkw])
        Vd = C.din(f"Vw{gi}", [W, 2, 128], BF16).rearrange("(w dd) k e -> dd w k e", dd=d)
        for rho in range(d):
            C.dma("sp", Vw[:, rho * ncw:(rho + 1) * ncw], Vd[rho].rearrange("(cw p) k e -> p cw k e", p=128),
                  [], [tvw])
        for rho in range(d):
            for ut in range(nut):
                t0 = rho + d * 128 * ut
                qsl = slice(t0, t0 + 127 * d + 1, d)
                blks = sorted(set([t0 // TB, (t0 + d * 127) // TB]))
                blks = list(range(blks[0], blks[-1] + 1))
                ob = 2 + (ocnt % 2)
                lb = 4 + (ocnt % 2)
                ocnt += 1
                obank, to = C.ps[ob], C.pst[ob]
                lbank, tl = C.ps[lb], C.pst[lb]
                first_o, first_l = True, True
                for kv in range(2):
                    hs = slice(kv * 64, kv * 64 + 64)
                    for c in range(2):
                        sb = scnt % 2
                        scnt += 1
                        sbank, tsb = C.ps[sb], C.pst[sb]
                        k0 = rho + d * 128 * (ut + c)
                        C.mm(sbank[:, :], dmat[:, gi, c, :], bmat[:, 2 * gi + kv, 0, :], True, False, [tc], [tsb])
                        C.mm(sbank[:, :], dmat[:, gi, c, :], bmat[:, 2 * gi + kv, 1, :], False, False, [tc], [tsb])
                        C.mm(sbank[:, :], kTw[hs, k0:k0 + 127 * d + 1:d], qT[hs, 4 * gi:4 * gi + 4, qsl], False, True,
                             [tkw] + [tq[4 * gi + i][bb] for i in range(4) for bb in blks], [tsb])
                        bias = None
                        if ut == 0 and c == 0:
                            bias = C.emask[:, 2:3]
                        if ut == nut - 1 and c == 1:
                            bias = C.emask[:, 3:4]
                        pt, tp = PT[sb], tpt[sb]
                        C.act(pt, sbank[:, :], AF.Exp, [tsb, C.t_const], [tp], bias=bias)
                        last = (kv == 1 and c == 1)
                        C.mm(obank[:, :], Vw[:, rho * ncw + ut + c, kv, :], pt, first_o, last, [tp, tvw], [to])
                        first_o = False
                        for i in range(4):
                            C.mm(lbank[0:8, 0:128], oneh[:, 4 * kv + i, :], pt[:, i * 128:(i + 1) * 128],
                                 first_l, last and i == 3, [tp, tc], [tl])
                            first_l = False
                av = accO[:, :, qsl]
                C.tt("dve", av, obank[:, :].rearrange("p (i q) -> p i q", i=4), av, ALU.add, [to, taccO], [taccO])
                lv = accL[0:8, qsl]
                C.tt("dve", lv, lbank[0:8, 0:128], lv, ALU.add, [tl, taccL], [taccL])
    P.barrier()
    m.release(mk1)
    wo = v3(m.bf16(4 * D), 4)
    two = Tok("wo")
    wod = C.din("wo", [512, D]).rearrange("(kc p) n -> p kc n", p=128)
    load_w(C, wo, wod, two)
    fin = alloc_fin_tmp(C, 4)
    for b in range(NB):
        sl = slice(b * TB, (b + 1) * TB)
        emit_finalize_outproj(C, layer, b, 8, 4, lambda fc, sl=sl: accO[:, fc, sl], taccO, accL[0:8, sl], taccL,
                              sel, wo, two, fin)
    P.barrier()
    m.release(mk0)


def emit_ffn(C, layer):
    m = C.mem
    P = C.P
    mk0 = m.mark()
    hT, th = emit_h(C, layer, 1)
    wg = [v3(m.bf16(KC * 512), KC) for _ in range(2)]
    wu = [v3(m.bf16(KC * 512), KC) for _ in range(2)]
    wo = [v3(m.bf16(4 * D), 4) for _ in range(2)]
    tw = [Tok("ffw0"), Tok("ffw1")]
    actb = [v3(m.bf16(4 * TB), 4) for _ in range(2)]
    tact = [[Tok(f"act{i}_{c}") for c in range(4)] for i in range(2)]
    sg = [m.f32(TB), m.f32(TB)]
    tsg = [Tok("sg0"), Tok("sg1")]
    wind = C.din("w_in", [D, 2 * DFF]).rearrange("(kc p) n -> p kc n", p=128)
    woutd = C.din("w_out", [DFF, D]).rearrange("(c p) n -> p c n", p=128)
    g2 = mod_vec(C, layer, 5)

    def load_group(gidx):
        c0, G = FF_GROUPS[gidx]
        i = gidx % 2
        load_w(C, wg[i][:, :, 0:G * 128], wind[:, :, c0 * 128:(c0 + G) * 128], tw[i])
        load_w(C, wu[i][:, :, 0:G * 128], wind[:, :, DFF + c0 * 128:DFF + (c0 + G) * 128], tw[i])
        load_w(C, wo[i][:, 0:G, :], woutd[:, c0:c0 + G, :], tw[i])

    load_group(0)
    cnt = 0
    ab = 0
    for gidx, (c0, G) in enumerate(FF_GROUPS):
        if gidx + 1 < len(FF_GROUPS):
            load_group(gidx + 1)
        i = gidx % 2
        for b in range(NB):
            sl = slice(b * TB, (b + 1) * TB)
            a = actb[ab % 2]
            ta = tact[ab % 2]
            ab += 1
            for c in range(G):
                gb, ub = 0 + (cnt % 2), 2 + (cnt % 2)
                s = cnt % 2
                cnt += 1
                for kc in range(KC):
                    C.mm(C.ps[gb][:, :], wg[i][:, kc, c * 128:(c + 1) * 128], hT[:, kc, sl], kc == 0, kc == KC - 1,
                         [tw[i], th[kc][b]], [C.pst[gb]])
                for kc in range(KC):
                    C.mm(C.ps[ub][:, :], wu[i][:, kc, c * 128:(c + 1) * 128], hT[:, kc, sl], kc == 0, kc == KC - 1,
                         [tw[i], th[kc][b]], [C.pst[ub]])
                C.act(sg[s], C.ps[gb][:, :], AF.Silu, [C.pst[gb]], [tsg[s]])
                C.tt("dve", a[:, c, :], sg[s], C.ps[ub][:, :], ALU.mult, [tsg[s], C.pst[ub]], [ta[c]])
            for mch in range(KC):
                yb = 4 + (mch % 4)
                for c in range(G):
                    C.mm(C.ps[yb][:, :], wo[i][:, c, mch * 128:(mch + 1) * 128], a[:, c, :], c == 0, c == G - 1,
                         [tw[i], ta[c]], [C.pst[yb]])
                xs = C.xT[:, mch, sl]
                C.stt(xs, C.ps[yb][:, :], g2[:, mch:mch + 1], xs, ALU.mult, ALU.add,
                      [C.pst[yb], C.tx[mch][b], C.t_const], [C.tx[mch][b]])
    P.barrier()
    m.release(mk0)


def emit_store_x(C):
    xd = C.dout("xT_out", [D, NT]).rearrange("(kc p) t -> p kc t", p=128)
    for kc in range(KC):
        for b in range(NB):
            C.dma("sp", xd[:, kc, b * TB:(b + 1) * TB], C.xT[:, kc, b * TB:(b + 1) * TB], [C.tx[kc][b]], [C.tx[kc][b]])
            C.out_toks.append(C.tx[kc][b])


def emit_final(C):
    m = C.mem
    od = C.dout("outT", [D, NT]).rearrange("(kc p) t -> p kc t", p=128)
    obuf = [m.f32(TB) for _ in range(4)]
    tob = [Tok(f"ob{i}") for i in range(4)]
    st = {"n": 0}

    def dst(kc, b):
        return obuf[st["n"] % 4]

    def dtok(kc, b):
        return tob[st["n"] % 4]

    def after(kc, b):
        i = st["n"] % 4
        C.dma("sp", od[:, kc, b * TB:(b + 1) * TB], obuf[i], [tob[i]], [tob[i]])
        st["n"] += 1

    emit_norm(C, C.gfin, None, dst, dtok, after_blk=after)
    C.out_toks += tob


def build_mod():
    nc = bass.Bass("TRN2", target_bir_lowering=False)
    C = Ctx(nc)
    m = C.mem
    cT = m.f32(8)
    cb = m.bf16(8)
    bsl = m.f32(24)
    res = m.f32(24)
    w = v3(m.bf16(KC * 3072), KC)
    tcn, tw, tr = Tok("c"), Tok("w"), Tok("res")
    C.dma("sp", cT, C.din("cT", [128, 8]), [], [tcn])
    C.dma("sp", bsl, C.din("bsl", [128, 24]), [], [tcn])
    wd = C.din("wsl", [D, 3072]).rearrange("(kc p) n -> p kc n", p=128)
    for kc in range(KC):
        load_w(C, w[:, kc, :], wd[:, kc, :], tw)
    C.act(cb, cT, AF.Silu, [tcn], [tcn])
    for j in range(24):
        for kc in range(KC):
            C.mm(C.ps[0][:, j:j + 1], w[:, kc, j * 128:(j + 1) * 128], cb[:, kc:kc + 1], kc == 0, kc == KC - 1,
                 [tw, tcn], [C.pst[0]])
    C.tt("dve", res, C.ps[0][:, 0:24], bsl, ALU.add, [C.pst[0], tcn], [tr])
    C.dma("sp", C.dout("modc", [128, 24]), res, [tr], [tr])
    C.P.final_wait("sp", [tr])
    C.P.emit()
    return nc


def build_prog(kind):
    nc = bass.Bass("TRN2", target_bir_lowering=False)
    C = Ctx(nc)
    setup_common(C)
    if kind == "kv0":
        C.kv_layer = 0
        emit_kv(C, "A")
    else:
        typ = "A" if kind == "mainA" else "B"
        if typ == "A":
            emit_attn_A(C, 0)
        else:
            emit_attn_B(C, 0)
        emit_ffn(C, 0)
        if kind == "mainBf":
            emit_final(C)
        else:
            C.kv_layer = 1
            emit_kv(C, "B" if typ == "A" else "A")
            emit_store_x(C)
    C.P.final_wait("sp", C.out_toks)
    C.P.emit()
    return nc


def _slopes(n):
    return (2.0 ** (-8.0 * np.arange(1, n + 1) / n)).astype(np.float32)


def _hilo(v):
    v = np.asarray(v, np.float32)
    hi = v.astype(NPBF)
    lo = (v - hi.astype(np.float32)).astype(NPBF)
    return hi, lo


def _consts_A():
    sl = _slopes(16)
    bmat = np.zeros((128, 4, 2, 512), NPBF)
    eye = np.eye(128, dtype=np.float32)
    for g in range(4):
        for i in range(4):
            hi, lo = _hilo(sl[4 * g + i])
            bmat[:, g, 0, i * 128:(i + 1) * 128] = (eye * np.float32(hi)).astype(NPBF)
            bmat[:, g, 1, i * 128:(i + 1) * 128] = (eye * np.float32(lo)).astype(NPBF)
    dmat = np.zeros((128, 3, 128), np.float32)
    q = np.arange(128)[:, None]
    j = np.arange(128)[None, :]
    for c in range(3):
        dist = np.abs(128 * (1 - c) + q - j)
        dmat[:, c, :] = np.where(dist <= 128, -dist, -32768.0)
    oneh = np.zeros((128, 16, 16), NPBF)
    for r in range(16):
        oneh[:, r, r] = 1.0
    sel = np.zeros((16, 8, 128), NPBF)
    for g in range(4):
        for i in range(4):
            fc = 4 * (g // 2) + i
            sel[4 * g + i, fc, (g % 2) * 64:(g % 2) * 64 + 64] = 1.0
    return bmat, dmat.astype(NPBF), oneh, sel


def _consts_B():
    sl = _slopes(24).reshape(3, 2, 4)
    bmat = np.zeros((128, 6, 2, 512), NPBF)
    eye = np.eye(128, dtype=np.float32)
    for gi in range(3):
        for kv in range(2):
            for i in range(4):
                hi, lo = _hilo(sl[gi, kv, i])
                bmat[:, 2 * gi + kv, 0, i * 128:(i + 1) * 128] = (eye * np.float32(hi)).astype(NPBF)
                bmat[:, 2 * gi + kv, 1, i * 128:(i + 1) * 128] = (eye * np.float32(lo)).astype(NPBF)
    dmat = np.zeros((128, 3, 2, 128), np.float32)
    q = np.arange(128)[:, None]
    j = np.arange(128)[None, :]
    for gi, d in enumerate(B_DIL):
        for c in range(2):
            rel = np.abs(q - j + 64 - 128 * c)
            dmat[:, gi, c, :] = np.where(rel <= 64, -(d * rel), -32768.0)
    oneh = np.zeros((128, 8, 8), NPBF)
    for r in range(8):
        oneh[:, r, r] = 1.0
    sel = np.zeros((8, 4, 128), NPBF)
    for kv in range(2):
        for i in range(4):
            sel[4 * kv + i, i, kv * 64:kv * 64 + 64] = 1.0
    return bmat, dmat.astype(NPBF), oneh, sel


def _fm(v):
    v = np.asarray(v, np.float32)
    return np.ascontiguousarray(v.reshape(-1, 128).T)


_PROGS = {}


def _prog(kind):
    if kind not in _PROGS:
        _PROGS[kind] = build_mod() if kind == "mod" else build_prog(kind)
    return _PROGS[kind]


def _run(kind, in_maps):
    res = run_bass_kernel_spmd(_prog(kind), in_maps, core_ids=list(range(NCORE)))
    return res.results


def _windows(parts, H, axis):
    full = np.concatenate(parts, axis=axis)
    pad = [(0, 0)] * full.ndim
    pad[axis] = (H, H)
    full = np.pad(full, pad)
    outs = []
    for r in range(NCORE):
        idx = [slice(None)] * full.ndim
        idx[axis] = slice(r * NT, r * NT + NT + 2 * H)
        outs.append(np.ascontiguousarray(full[tuple(idx)]))
    return outs


def kernel(x, c, ada_w, ada_b, norm_mix, norm_ffn, ffn_w_in, ffn_w_out, a_w_in, a_w_out, a_sink, b_w_in, b_w_out,
           final_norm, _debug=None):
    f = lambda a: np.asarray(a, dtype=np.float32)
    x, c, ada_w, ada_b = f(x), f(c), f(ada_w), f(ada_b)
    norm_mix, norm_ffn, ffn_w_in, ffn_w_out = f(norm_mix), f(norm_ffn), f(ffn_w_in), f(ffn_w_out)
    a_w_in, a_w_out, a_sink, b_w_in, b_w_out, final_norm = (f(a_w_in), f(a_w_out), f(a_sink), f(b_w_in),
                                                            f(b_w_out), f(final_norm))
    cT = _fm(c[0])
    maps = []
    for r in range(NCORE):
        i, h = r // 2, r % 2
        maps.append({"cT": cT, "wsl": np.ascontiguousarray(ada_w[i][:, h * 3072:(h + 1) * 3072]),
                     "bsl": _fm(ada_b[i][h * 3072:(h + 1) * 3072])})
    res = _run("mod", maps)
    modT = np.concatenate([np.asarray(res[r]["modc"], np.float32) for r in range(NCORE)], axis=1)
    gainT = np.concatenate([_fm(norm_mix[i]) for i in range(4)] + [_fm(norm_ffn[i]) for i in range(4)]
                           + [_fm(final_norm)], axis=1)

    def rot(i):
        order = [(i + k) % 4 for k in range(4)]
        mt = np.concatenate([modT[:, o * 48:(o + 1) * 48] for o in order], axis=1)
        gt = np.concatenate([gainT[:, o * 8:(o + 1) * 8] for o in order]
                            + [gainT[:, 32 + o * 8:32 + (o + 1) * 8] for o in order] + [gainT[:, 64:72]], axis=1)
        return np.ascontiguousarray(mt), np.ascontiguousarray(gt)

    emasks = []
    for r in range(NCORE):
        e = np.zeros((128, 4), np.float32)
        if r == 0:
            e[:, 0] = MASKV
            e[:64, 2] = MASKV
        if r == NCORE - 1:
            e[:, 1] = MASKV
            e[64:, 3] = MASKV
        emasks.append(e)
    xT = [np.ascontiguousarray(x[0, r * NT:(r + 1) * NT, :].T) for r in range(NCORE)]

    def a_q_perm():
        cols = []
        for hc in range(8):
            for half in range(2):
                g, i = 2 * (hc // 4) + half, hc % 4
                h = 4 * g + i
                cols.append(np.arange(h * 64, h * 64 + 64))
        return np.concatenate(cols)

    def b_q_perm():
        cols = []
        for gi in range(3):
            for i in range(4):
                for kv in range(2):
                    h = kv * 4 + i
                    cols.append(gi * 512 + np.arange(h * 64, h * 64 + 64))
        return np.concatenate(cols)

    def b_o_perm():
        rows = []
        for i in range(4):
            for kv in range(2):
                h = kv * 4 + i
                rows.append(np.arange(h * 64, h * 64 + 64))
        return np.concatenate(rows)

    aqp, bqp, bop = a_q_perm(), b_q_perm(), b_o_perm()
    cA, cB = _consts_A(), _consts_B()

    def kv_weights(i):
        j = i // 2
        if i % 2 == 0:
            return (np.ascontiguousarray(a_w_in[j][:, 1024:1280]), np.ascontiguousarray(a_w_in[j][:, 1280:1536]))
        return (np.ascontiguousarray(b_w_in[j][:, 1536:1920]), np.ascontiguousarray(b_w_in[j][:, 1920:2304]))

    mt, gt = rot(0)
    wk, wv = kv_weights(0)
    maps = [{"xT": xT[r], "modT": mt, "gainT": gt, "emask": emasks[r], "wk": wk, "wv": wv} for r in range(NCORE)]
    res = _run("kv0", maps)
    KT = [np.asarray(res[r]["KT"]) for r in range(NCORE)]
    Vp = [np.asarray(res[r]["Vp"]) for r in range(NCORE)]
    dbg = {}
    for i in range(4):
        j = i // 2
        mt, gt = rot(i)
        base = {"modT": mt, "gainT": gt, "w_in": ffn_w_in[i], "w_out": ffn_w_out[i]}
        if i % 2 == 0:
            kind = "mainA"
            KTw = _windows(KT, 128, 1)
            Vw = _windows(Vp, 128, 0)
            base.update({"wq": np.ascontiguousarray(a_w_in[j][:, :1024][:, aqp]),
                         "wo": np.ascontiguousarray(a_w_out[j][aqp, :]),
                         "sink": np.ascontiguousarray(a_sink[j].reshape(16, 1)),
                         "bmat": cA[0], "dmat": cA[1], "oneh": cA[2], "sel": cA[3]})
            per = [{"KTw": KTw[r], "Vw": Vw[r]} for r in range(NCORE)]
        else:
            kind = "mainB" if i < 3 else "mainBf"
            base.update({"wq": np.ascontiguousarray(b_w_in[j][:, :1536][:, bqp]),
                         "wo": np.ascontiguousarray(b_w_out[j][bop, :]),
                         "bmat": cB[0], "dmat": cB[1], "oneh": cB[2], "sel": cB[3]})
            per = [dict() for _ in range(NCORE)]
            for gi, d in enumerate(B_DIL):
                KTw = _windows([k[gi * 128:(gi + 1) * 128] for k in KT], 64 * d, 1)
                Vw = _windows([v[:, 2 * gi:2 * gi + 2] for v in Vp], 64 * d, 0)
                for r in range(NCORE):
                    per[r][f"KTw{gi}"] = KTw[r]
                    per[r][f"Vw{gi}"] = Vw[r]
        if i < 3:
            wk, wv = kv_weights(i + 1)
            base.update({"wk": wk, "wv": wv})
        maps = []
        for r in range(NCORE):
            mp = dict(base)
            mp.update(per[r])
            mp["xT"] = xT[r]
            mp["emask"] = emasks[r]
            maps.append(mp)
        res = _run(kind, maps)
        if i < 3:
            xT = [np.asarray(res[r]["xT_out"], np.float32) for r in range(NCORE)]
            KT = [np.asarray(res[r]["KT"]) for r in range(NCORE)]
            Vp = [np.asarray(res[r]["Vp"]) for r in range(NCORE)]
            if _debug is not None:
                _debug[f"x{i}"] = np.concatenate([t.T for t in xT], axis=0)
        else:
            out = np.concatenate([np.asarray(res[r]["outT"], np.float32).T for r in range(NCORE)], axis=0)
    return np.ascontiguousarray(out.reshape(1, SEQ, D).astype(np.float32))
```

```python
import contextlib
import numpy as np
import ml_dtypes
import concourse.bass as bass
import concourse.mybir as mybir
from concourse.bass_utils import run_bass_kernel_spmd

F32 = mybir.dt.float32
BF16 = mybir.dt.bfloat16
AF = mybir.ActivationFunctionType
ALU = mybir.AluOpType
NPBF = ml_dtypes.bfloat16

NCORE = 8
NT = 2048
SEQ = 16384
D = 1024
KC = 8
TB = 512
NB = 4
DFF = 2816
NFF = 22
EPS = 1e-6
MASKV = -1.0e4
FF_GROUPS = [(0, 4), (4, 4), (8, 4), (12, 4), (16, 3), (19, 3)]
B_DIL = (1, 4, 16)
ARENA_WORDS = 52000

ENGS = ("pe", "act", "dve", "pool", "sp")


class Tok:
    __slots__ = ("name", "writer", "readers")

    def __init__(self, name):
        self.name = name
        self.writer = None
        self.readers = []


class Prog:
    def __init__(self, nc, n_dma_sems=24):
        self.nc = nc
        self.q = {e: [] for e in ENGS}
        self.cnt = {e: 0 for e in ENGS}
        self.waited = {e: {} for e in ENGS}
        self.n_dma_sems = n_dma_sems
        self.dma_cnt = [0] * n_dma_sems
        self.dma_rr = 0
        self.sems = {}
        self.ncc = 0

    def _need(self, eng, deps):
        w = self.waited[eng]
        out = {}
        for d in deps:
            if d is None:
                continue
            k, v = d
            if w.get(k, 0) >= v:
                continue
            if out.get(k, 0) < v:
                out[k] = v
        for k, v in out.items():
            w[k] = v
        return list(out.items())

    def _deps(self, reads, writes):
        deps = []
        for t in reads:
            deps.append(t.writer)
        for t in writes:
            deps.append(t.writer)
            deps.extend(t.readers)
        return deps

    def _mark(self, me, reads, writes):
        for t in reads:
            t.readers.append(me)
        for t in writes:
            t.writer = me
            t.readers = []

    def op(self, eng, fn, reads=(), writes=(), signal=True):
        deps = self._deps(reads, writes)
        if eng == "pe":
            deps = [d for d in deps if d is not None and d[0] != "pe"]
        waits = self._need(eng, deps)
        if signal:
            self.cnt[eng] += 1
            me = (eng, self.cnt[eng])
            self.q[eng].append((waits, fn, (eng, 1)))
        else:
            me = (eng, self.cnt[eng] + 1)
            self.q[eng].append((waits, fn, "nosig"))
        self._mark(me, reads, writes)
        return me

    def dma(self, eng, fn, reads=(), writes=()):
        deps = self._deps(reads, writes)
        k = self.dma_rr
        self.dma_rr = (self.dma_rr + 1) % self.n_dma_sems
        key = ("dma", k)
        if self.dma_cnt[k] > 0:
            deps.append((key, self.dma_cnt[k]))
        waits = self._need(eng, deps)
        self.dma_cnt[k] += 16
        me = (key, self.dma_cnt[k])
        self.q[eng].append((waits, fn, (key, 16)))
        self._mark(me, reads, writes)
        return me

    def coll(self, fn, reads=(), writes=()):
        deps = self._deps(reads, writes)
        waits = self._need("pool", deps)
        key = ("cc", self.ncc)
        self.ncc += 1
        me = (key, 1)
        self.q["pool"].append((waits, fn, (key, None)))
        self._mark(me, reads, writes)
        return me

    def barrier(self):
        for e in ENGS:
            deps = [(x, self.cnt[x]) for x in ENGS if x != e and self.cnt[x] > 0]
            deps += [(("dma", k), self.dma_cnt[k]) for k in range(self.n_dma_sems) if self.dma_cnt[k] > 0]
            deps += [(("cc", k), 1) for k in range(self.ncc)]
            waits = self._need(e, deps)
            if waits:
                self.q[e].append((waits, None, None))

    def final_wait(self, eng, toks):
        deps = [t.writer for t in toks]
        waits = self._need(eng, deps)
        self.q[eng].append((waits, None, None))

    def emit(self):
        nc = self.nc
        with contextlib.ExitStack() as st:
            for e in ENGS:
                self.sems[e] = st.enter_context(nc.semaphore(f"s_{e}"))
            for k in range(self.n_dma_sems):
                self.sems[("dma", k)] = st.enter_context(nc.semaphore(f"s_dma{k}"))
            for k in range(self.ncc):
                self.sems[("cc", k)] = st.enter_context(nc.semaphore(f"s_cc{k}"))
            block = st.enter_context(nc.Block())

            def run(engname):
                def body(engine):
                    for waits, fn, inc in self.q[engname]:
                        for k, v in waits:
                            engine.wait_ge(self.sems[k], v)
                        if fn is not None:
                            ins = fn(engine)
                            if inc == "nosig":
                                continue
                            if inc[1] is None:
                                ins.then_inc(self.sems[inc[0]])
                            else:
                                ins.then_inc(self.sems[inc[0]], inc[1])
                return body

            block.tensor(run("pe"))
            block.scalar(run("act"))
            block.vector(run("dve"))
            block.gpsimd(run("pool"))
            block.sync(run("sp"))


class Mem:
    def __init__(self, nc, words):
        self.arena = nc.alloc_sbuf_tensor("arena", [128, words], F32)
        self.words = words
        self.top = 0

    def mark(self):
        return self.top

    def release(self, m):
        self.top = m

    def f32(self, n):
        off = self.top
        self.top += (n + 7) // 8 * 8
        assert self.top <= self.words, f"SBUF arena overflow {self.top} > {self.words}"
        return self.arena[:, off:off + n]

    def bf16(self, n):
        w = (n + 1) // 2
        off = self.top
        self.top += (w + 7) // 8 * 8
        assert self.top <= self.words, f"SBUF arena overflow {self.top} > {self.words}"
        return self.arena[:, off:off + w].bitcast(BF16)[:, 0:n]


class Ctx:
    def __init__(self, nc):
        self.nc = nc
        self.P = Prog(nc)
        self.mem = Mem(nc, ARENA_WORDS)
        self.ps = [nc.alloc_psum_tensor(f"psb{b}", [128, 512], F32) for b in range(8)]
        self.pst = [Tok(f"ps{b}") for b in range(8)]
        self.dram = {}
        self.out_toks = []

    def din(self, name, shape, dt=F32):
        t = self.nc.dram_tensor(name, list(shape), dt, kind="ExternalInput").ap()
        self.dram[name] = t
        return t

    def dout(self, name, shape, dt=F32):
        t = self.nc.dram_tensor(name, list(shape), dt, kind="ExternalOutput").ap()
        self.dram[name] = t
        return t

    def mm(self, out, lhsT, rhs, start, stop, reads, writes):
        self.P.op("pe", lambda e: e.matmul(out, lhsT=lhsT, rhs=rhs, start=start, stop=stop), reads, writes,
                  signal=bool(stop))

    def act(self, out, in_, func, reads, writes, bias=None, scale=None):
        kw = {}
        if bias is not None:
            kw["bias"] = bias
        if scale is not None:
            kw["scale"] = scale
        self.P.op("act", lambda e: e.activation(out=out, in_=in_, func=func, **kw), reads, writes)

    def tt(self, eng, out, in0, in1, op, reads, writes):
        self.P.op(eng, lambda e: e.tensor_tensor(out=out, in0=in0, in1=in1, op=op), reads, writes)

    def ts(self, eng, out, in0, s1, s2, op0, op1, reads, writes):
        if op1 is None:
            self.P.op(eng, lambda e: e.tensor_scalar(out=out, in0=in0, scalar1=s1, scalar2=None, op0=op0), reads, writes)
        else:
            self.P.op(eng, lambda e: e.tensor_scalar(out=out, in0=in0, scalar1=s1, scalar2=s2, op0=op0, op1=op1),
                      reads, writes)

    def stt(self, out, in0, scalar, in1, op0, op1, reads, writes):
        self.P.op("dve", lambda e: e.scalar_tensor_tensor(out=out, in0=in0, scalar=scalar, in1=in1, op0=op0, op1=op1),
                  reads, writes)

    def copy(self, eng, out, in_, reads, writes):
        if eng == "act":
            self.P.op("act", lambda e: e.activation(out=out, in_=in_, func=AF.Copy), reads, writes)
        else:
            self.P.op(eng, lambda e: e.tensor_copy(out=out, in_=in_), reads, writes)

    def memset(self, eng, ap, val, writes):
        self.P.op(eng, lambda e: e.memset(ap, val), (), writes)

    def dma(self, q, out, in_, reads, writes):
        self.P.dma(q, lambda e: e.dma_start(out=out, in_=in_), reads, writes)


def v3(ap, a):
    return ap.rearrange("p (a b) -> p a b", a=a)


def v4(ap, a, b):
    return ap.rearrange("p (a b c) -> p a b c", a=a, b=b)


def setup_common(C, need_x=True):
    m = C.mem
    if need_x:
        C.xT = v3(m.f32(KC * NT), KC)
        C.tx = [[Tok(f"x{kc}_{b}") for b in range(NB)] for kc in range(KC)]
        xd = C.din("xT", [D, NT]).rearrange("(kc p) t -> p kc t", p=128)
        for kc in range(KC):
            for b in range(NB):
                C.dma("sp", C.xT[:, kc, b * TB:(b + 1) * TB], xd[:, kc, b * TB:(b + 1) * TB], [], [C.tx[kc][b]])
    C.mod = m.f32(192)
    C.gain = m.f32(72)
    C.emask = m.f32(4)
    C.gsc = m.f32(64)
    C.gfin = m.f32(8)
    C.ones = m.bf16(128)
    C.t_const = Tok("const")
    tmp = m.f32(8)
    C.dma("sp", C.mod, C.din("modT", [128, 192]), [], [C.t_const])
    C.dma("sp", C.gain, C.din("gainT", [128, 72]), [], [C.t_const])
    C.dma("sp", C.emask, C.din("emask", [128, 4]), [], [C.t_const])
    C.memset("pool", C.ones, 1.0, [C.t_const])
    C.epsb = m.f32(1)
    C.memset("pool", C.epsb, 1024.0 * EPS, [C.t_const])
    for i in range(4):
        for w in range(2):
            sc = C.mod[:, i * 48 + (3 * w + 1) * 8: i * 48 + (3 * w + 1) * 8 + 8]
            g = C.gain[:, (w * 4 + i) * 8:(w * 4 + i) * 8 + 8]
            o = C.gsc[:, (i * 2 + w) * 8:(i * 2 + w) * 8 + 8]
            C.ts("dve", tmp, sc, 1.0, 32.0, ALU.add, ALU.mult, [C.t_const], [C.t_const])
            C.tt("dve", o, tmp, g, ALU.mult, [C.t_const], [C.t_const])
    C.ts("dve", C.gfin, C.gain[:, 64:72], 32.0, None, ALU.mult, None, [C.t_const], [C.t_const])


def mod_vec(C, i, v):
    return C.mod[:, i * 48 + v * 8: i * 48 + v * 8 + 8]


def emit_norm(C, gsc, sh, dst_fn, dst_tok_fn, after_blk=None):
    m = C.mem
    sq = [v3(m.bf16(KC * TB), KC) for _ in range(2)]
    tsq = [[Tok(f"sq{i}_{kc}") for kc in range(KC)] for i in range(2)]
    rstd = [m.f32(TB) for _ in range(2)]
    trs = [Tok("rstd0"), Tok("rstd1")]
    y = [m.f32(TB) for _ in range(2)]
    ty = [Tok("y0"), Tok("y1")]
    bank, tb = C.ps[7], C.pst[7]
    for b in range(NB):
        s = sq[b % 2]
        ts_ = tsq[b % 2]
        sl = slice(b * TB, (b + 1) * TB)
        for kc in range(KC):
            C.act(s[:, kc, :], C.xT[:, kc, sl], AF.Square, [C.tx[kc][b]], [ts_[kc]])
        for kc in range(KC):
            C.mm(bank[:, :], C.ones, s[:, kc, :], kc == 0, kc == KC - 1, [ts_[kc], C.t_const], [tb])
        r = rstd[b % 2]
        C.act(r, bank[:, :], AF.Ln, [tb, C.t_const], [trs[b % 2]], bias=C.epsb[:, 0:1])
        C.act(r, r, AF.Exp, [trs[b % 2]], [trs[b % 2]], scale=-0.5)
        for kc in range(KC):
            yy = y[kc % 2]
            C.tt("dve", yy, C.xT[:, kc, sl], r, ALU.mult, [C.tx[kc][b], trs[b % 2]], [ty[kc % 2]])
            if sh is not None:
                C.act(dst_fn(kc, b), yy, AF.Identity, [ty[kc % 2], C.t_const], [dst_tok_fn(kc, b)],
                      bias=sh[:, kc:kc + 1], scale=gsc[:, kc:kc + 1])
            else:
                C.act(dst_fn(kc, b), yy, AF.Identity, [ty[kc % 2], C.t_const], [dst_tok_fn(kc, b)],
                      scale=gsc[:, kc:kc + 1])
            if after_blk is not None:
                after_blk(kc, b)


def load_w(C, dst, src, tok):
    C.dma("pool", dst, src, [], [tok])


def emit_h(C, layer, which):
    hT = v3(C.mem.bf16(KC * NT), KC)
    th = [[Tok(f"h{kc}_{b}") for b in range(NB)] for kc in range(KC)]
    gsc = C.gsc[:, (layer * 2 + which) * 8:(layer * 2 + which) * 8 + 8]
    sh = mod_vec(C, layer, 3 * which)
    emit_norm(C, gsc, sh, lambda kc, b: hT[:, kc, b * TB:(b + 1) * TB], lambda kc, b: th[kc][b])
    return hT, th


def emit_kv(C, typ):
    m = C.mem
    mk = m.mark()
    CK = 256 if typ == "A" else 384
    nch, nkv = CK // 128, CK // 64
    layer = C.kv_layer
    wk = v3(m.bf16(KC * CK), KC)
    wv = v3(m.bf16(KC * CK), KC)
    twk, twv = Tok("wk"), Tok("wv")
    wkd = C.din("wk", [D, CK]).rearrange("(kc p) n -> p kc n", p=128)
    wvd = C.din("wv", [D, CK]).rearrange("(kc p) n -> p kc n", p=128)
    load_w(C, wk, wkd, twk)
    load_w(C, wv, wvd, twv)
    kT = v3(m.bf16(nch * NT), nch)
    tk = [Tok(f"kT{c}") for c in range(nch)]
    Vp = m.bf16(16 * nkv * 128)
    tv = Tok("Vp")
    C.memset("pool", Vp, 0.0, [tv])
    Vp5 = Vp.rearrange("p (t k two d) -> p t k two d", t=16, k=nkv // 2, two=2)
    hT, th = emit_h(C, layer, 0)
    KTd = C.dout("KT", [CK, NT], BF16)
    Vpd = C.dout("Vp", [NT, nkv, 128], BF16)
    cnt = 0
    for b in range(NB):
        sl = slice(b * TB, (b + 1) * TB)
        for ch in range(nch):
            bk = 5 + (cnt % 2)
            cnt += 1
            for kc in range(KC):
                C.mm(C.ps[bk][:, :], wk[:, kc, ch * 128:(ch + 1) * 128], hT[:, kc, sl], kc == 0, kc == KC - 1,
                     [twk, th[kc][b]], [C.pst[bk]])
            C.copy("act", kT[:, ch, sl], C.ps[bk][:, :], [C.pst[bk]], [tk[ch]])
    for ch in range(nch):
        C.dma("sp", KTd[ch * 128:(ch + 1) * 128, :], kT[:, ch, :], [tk[ch]], [tk[ch]])
    for tt_ in range(16):
        b = tt_ // 4
        bk = 5 + (cnt % 2)
        cnt += 1
        for kc in range(KC):
            C.mm(C.ps[bk][:, 0:CK], hT[:, kc, tt_ * 128:(tt_ + 1) * 128], wv[:, kc, :], kc == 0, kc == KC - 1,
                 [twv, th[kc][b]], [C.pst[bk]])
        pv = C.ps[bk][:, 0:CK].rearrange("p (k two d) -> p k two d", k=nkv // 2, two=2)
        C.copy("dve", Vp5[:, tt_, :, 0, 0:64], pv[:, :, 0, :], [C.pst[bk]], [tv])
        C.copy("act", Vp5[:, tt_, :, 1, 64:128], pv[:, :, 1, :], [C.pst[bk]], [tv])
    C.dma("sp", Vpd.rearrange("(t p) k d -> p t k d", p=128), Vp.rearrange("p (t k d) -> p t k d", t=16, k=nkv),
          [tv], [tv])
    C.out_toks += tk + [tv]
    C.P.barrier()
    m.release(mk)


def emit_q(C, layer, wq_name, ncol):
    m = C.mem
    nq = ncol // 128
    qT = v3(m.bf16(nq * NT), nq)
    tq = [[Tok(f"q{c}_{b}") for b in range(NB)] for c in range(nq)]
    mk = m.mark()
    wq = v3(m.bf16(KC * ncol), KC)
    twq = Tok("wq")
    wqd = C.din(wq_name, [D, ncol]).rearrange("(kc p) n -> p kc n", p=128)
    for kc in range(KC):
        load_w(C, wq[:, kc, :], wqd[:, kc, :], twq)
    hT, th = emit_h(C, layer, 0)
    cnt = 0
    for b in range(NB):
        sl = slice(b * TB, (b + 1) * TB)
        for c in range(nq):
            bk = 5 + (cnt % 2)
            cnt += 1
            for kc in range(KC):
                C.mm(C.ps[bk][:, :], wq[:, kc, c * 128:(c + 1) * 128], hT[:, kc, sl], kc == 0, kc == KC - 1,
                     [twq, th[kc][b]], [C.pst[bk]])
            C.act(qT[:, c, sl], C.ps[bk][:, :], AF.Copy, [C.pst[bk]], [tq[c][b]], scale=0.125)
    C.P.barrier()
    m.release(mk)
    return qT, tq


def emit_finalize_outproj(C, layer, b, R, nfc, accO_fn, taccO, accL_ap, taccL, sel, wo, two, tmp):
    rl, rh, rlo, rlbc, OTn, t_rl, t_rlbc, t_otn = tmp
    sl = slice(b * TB, (b + 1) * TB)
    C.P.op("dve", lambda e: e.reciprocal(out=rl[0:R, :], in_=accL_ap), [taccL], [t_rl])
    C.copy("dve", rh[0:R, :], rl[0:R, :], [t_rl], [t_rl])
    C.tt("dve", rlo[0:R, :], rl[0:R, :], rh[0:R, :], ALU.subtract, [t_rl], [t_rl])
    for fc in range(nfc):
        bk = 5 + (fc % 2)
        C.mm(C.ps[bk][:, :], sel[0:R, fc, :], rh[0:R, :], True, False, [t_rl, C.t_const], [C.pst[bk]])
        C.mm(C.ps[bk][:, :], sel[0:R, fc, :], rlo[0:R, :], False, True, [t_rl, C.t_const], [C.pst[bk]])
        C.copy("act", rlbc[fc % 2], C.ps[bk][:, :], [C.pst[bk]], [t_rlbc[fc % 2]])
        C.tt("dve", OTn[:, fc, :], accO_fn(fc), rlbc[fc % 2], ALU.mult, [taccO, t_rlbc[fc % 2]], [t_otn[fc]])
    g1 = mod_vec(C, layer, 2)
    for mch in range(KC):
        bk = 5 + (mch % 2)
        for fc in range(nfc):
            C.mm(C.ps[bk][:, :], wo[:, fc, mch * 128:(mch + 1) * 128], OTn[:, fc, :], fc == 0, fc == nfc - 1,
                 [two, t_otn[fc]], [C.pst[bk]])
        xs = C.xT[:, mch, sl]
        C.stt(xs, C.ps[bk][:, :], g1[:, mch:mch + 1], xs, ALU.mult, ALU.add,
              [C.pst[bk], C.tx[mch][b], C.t_const], [C.tx[mch][b]])


def alloc_fin_tmp(C, nfc):
    m = C.mem
    rl = m.f32(TB)
    rh = m.bf16(TB)
    rlo = m.bf16(TB)
    rlbc = [m.f32(TB), m.f32(TB)]
    OTn = v3(m.bf16(nfc * TB), nfc)
    return (rl, rh, rlo, rlbc, OTn, Tok("rl"), [Tok("rlbc0"), Tok("rlbc1")], [Tok(f"otn{f}") for f in range(nfc)])


def emit_attn_A(C, layer):
    m = C.mem
    P = C.P
    mk0 = m.mark()
    qT, tq = emit_q(C, layer, "wq", 1024)
    bmat = v4(m.bf16(4 * 2 * 512), 4, 2)
    dmat = v3(m.bf16(3 * 128), 3)
    oneh = v3(m.bf16(16 * 16), 16)
    sel = v3(m.bf16(8 * 128), 8)
    esink = m.f32(1)
    tc = Tok("attc")
    C.dma("sp", bmat, C.din("bmat", [128, 4, 2, 512], BF16), [], [tc])
    C.dma("sp", dmat, C.din("dmat", [128, 3, 128], BF16), [], [tc])
    C.dma("sp", oneh, C.din("oneh", [128, 16, 16], BF16), [], [tc])
    C.dma("sp", sel[0:16], C.din("sel", [16, 8, 128], BF16), [], [tc])
    C.dma("sp", esink[0:16, :], C.din("sink", [16, 1]), [], [tc])
    C.act(esink[0:16, :], esink[0:16, :], AF.Exp, [tc], [tc])
    wo = v3(m.bf16(KC * D), KC)
    two = Tok("wo")
    wod = C.din("wo", [D, D]).rearrange("(kc p) n -> p kc n", p=128)
    for kc in range(KC):
        load_w(C, wo[:, kc, :], wod[:, kc, :], two)
    kTw = [v3(m.bf16(2 * 768), 2) for _ in range(2)]
    Vw = [v4(m.bf16(6 * 4 * 128), 6, 4) for _ in range(2)]
    tkw = [Tok("kTw0"), Tok("kTw1")]
    tvw = [Tok("Vw0"), Tok("Vw1")]
    accO = v3(m.f32(8 * TB), 8)
    taccO = Tok("accO")
    accL = m.f32(TB)
    taccL = Tok("accL")
    PT = [m.bf16(512), m.bf16(512)]
    tpt = [Tok("pt0"), Tok("pt1")]
    fin = alloc_fin_tmp(C, 8)
    KTd = C.din("KTw", [256, NT + 256], BF16)
    Vd = C.din("Vw", [NT + 256, 4, 128], BF16).rearrange("(t p) k d -> p t k d", p=128)
    scnt = 0
    for b in range(NB):
        kb, vb = kTw[b % 2], Vw[b % 2]
        for ch in range(2):
            C.dma("sp", kb[:, ch, :], KTd[ch * 128:(ch + 1) * 128, b * TB:b * TB + 768], [], [tkw[b % 2]])
        C.dma("sp", vb, Vd[:, 4 * b:4 * b + 6], [], [tvw[b % 2]])
        for qi in range(4):
            j = 4 * b + qi
            qs = slice(b * TB + qi * 128, b * TB + (qi + 1) * 128)
            lbank, tl = C.ps[4], C.pst[4]
            first_l = True
            for pair in range(2):
                obank, to = C.ps[2 + pair], C.pst[2 + pair]
                first_o = True
                for g in (2 * pair, 2 * pair + 1):
                    half = g % 2
                    hs = slice(half * 64, half * 64 + 64)
                    for c in range(3):
                        sb = scnt % 2
                        scnt += 1
                        sbank, tsb = C.ps[sb], C.pst[sb]
                        C.mm(sbank[:, :], dmat[:, c, :], bmat[:, g, 0, :], True, False, [tc], [tsb])
                        C.mm(sbank[:, :], dmat[:, c, :], bmat[:, g, 1, :], False, False, [tc], [tsb])
                        C.mm(sbank[:, :], kb[hs, pair, (qi + c) * 128:(qi + c + 1) * 128],
                             qT[hs, 4 * pair:4 * pair + 4, qs], False, True,
                             [tkw[b % 2]] + [tq[4 * pair + i][b] for i in range(4)], [tsb])
                        bias = None
                        if j == 0 and c == 0:
                            bias = C.emask[:, 0:1]
                        if j == 15 and c == 2:
                            bias = C.emask[:, 1:2]
                        pt, tp = PT[sb], tpt[sb]
                        C.act(pt, sbank[:, :], AF.Exp, [tsb, C.t_const], [tp], bias=bias)
                        last_o = (g == 2 * pair + 1 and c == 2)
                        C.mm(obank[:, :], vb[:, qi + c, g, :], pt, first_o, last_o, [tp, tvw[b % 2]], [to])
                        first_o = False
                        for i in range(4):
                            last_l = (pair == 1 and last_o and i == 3)
                            C.mm(lbank[0:16, 0:128], oneh[:, 4 * g + i, :], pt[:, i * 128:(i + 1) * 128],
                                 first_l, last_l, [tp, tc], [tl])
                            first_l = False
                C.copy("act", accO[:, 4 * pair:4 * pair + 4, qi * 128:(qi + 1) * 128],
                       obank[:, :].rearrange("p (i q) -> p i q", i=4), [to], [taccO])
            C.ts("dve", accL[0:16, qi * 128:(qi + 1) * 128], lbank[0:16, 0:128], esink[0:16, 0:1], None,
                 ALU.add, None, [tl, tc], [taccL])
        emit_finalize_outproj(C, layer, b, 16, 8, lambda fc: accO[:, fc, :], taccO, accL[0:16, :], taccL,
                              sel, wo, two, fin)
    P.barrier()
    m.release(mk0)


def emit_attn_B(C, layer):
    m = C.mem
    P = C.P
    mk0 = m.mark()
    qT, tq = emit_q(C, layer, "wq", 1536)
    accO = v3(m.f32(4 * NT), 4)
    accL = m.f32(NT)
    taccO, taccL = Tok("accO"), Tok("accL")
    C.memset("pool", accO.rearrange("p a b -> p (a b)"), 0.0, [taccO])
    C.memset("pool", accL, 0.0, [taccL])
    oneh = v3(m.bf16(8 * 8), 8)
    sel = v3(m.bf16(4 * 128), 4)
    tc = Tok("attc")
    C.dma("sp", oneh, C.din("oneh", [128, 8, 8], BF16), [], [tc])
    C.dma("sp", sel[0:8], C.din("sel", [8, 4, 128], BF16), [], [tc])
    mk1 = m.mark()
    bmat = v4(m.bf16(6 * 2 * 512), 6, 2)
    dmat = v4(m.bf16(3 * 2 * 128), 3, 2)
    C.dma("sp", bmat, C.din("bmat", [128, 6, 2, 512], BF16), [], [tc])
    C.dma("sp", dmat, C.din("dmat", [128, 3, 2, 128], BF16), [], [tc])
    PT = [m.bf16(512), m.bf16(512)]
    tpt = [Tok("pt0"), Tok("pt1")]
    WMAX = NT + 128 * 16
    kTw_buf = m.bf16(WMAX)
    Vw_buf = m.bf16(32 * 2 * 128)
    tkw, tvw = Tok("kTw"), Tok("Vw")
    scnt = 0
    ocnt = 0
    for gi, d in enumerate(B_DIL):
        W = NT + 128 * d
        ncw = 16 // d + 1
        U = NT // d
        nut = U // 128
        kTw = kTw_buf[:, 0:W]
        Vw = v4(Vw_buf[:, 0:d * ncw * 256], d * ncw, 2)
        C.dma("sp", kTw, C.din(f"KTw{gi}", [128, W], BF16), [], [tkw])
        Vd = C.din(f"Vw{gi}", [W, 2, 128], BF16).rearrange("(w dd) k e -> dd w k e", dd=d)
        for rho in range(d):
            C.dma("sp", Vw[:, rho * ncw:(rho + 1) * ncw], Vd[rho].rearrange("(cw p) k e -> p cw k e", p=128),
                  [], [tvw])
        for rho in range(d):
            for ut in range(nut):
                t0 = rho + d * 128 * ut
                qsl = slice(t0, t0 + 127 * d + 1, d)
                blks = sorted(set([t0 // TB, (t0 + d * 127) // TB]))
                blks = list(range(blks[0], blks[-1] + 1))
                ob = 2 + (ocnt % 2)
                lb = 4 + (ocnt % 2)
                ocnt += 1
                obank, to = C.ps[ob], C.pst[ob]
                lbank, tl = C.ps[lb], C.pst[lb]
                first_o, first_l = True, True
                for kv in range(2):
                    hs = slice(kv * 64, kv * 64 + 64)
                    for c in range(2):
                        sb = scnt % 2
                        scnt += 1
                        sbank, tsb = C.ps[sb], C.pst[sb]
                        k0 = rho + d * 128 * (ut + c)
                        C.mm(sbank[:, :], dmat[:, gi, c, :], bmat[:, 2 * gi + kv, 0, :], True, False, [tc], [tsb])
                        C.mm(sbank[:, :], dmat[:, gi, c, :], bmat[:, 2 * gi + kv, 1, :], False, False, [tc], [tsb])
                        C.mm(sbank[:, :], kTw[hs, k0:k0 + 127 * d + 1:d], qT[hs, 4 * gi:4 * gi + 4, qsl], False, True,
                             [tkw] + [tq[4 * gi + i][bb] for i in range(4) for bb in blks], [tsb])
                        bias = None
                        if ut == 0 and c == 0:
                            bias = C.emask[:, 2:3]
                        if ut == nut - 1 and c == 1:
                            bias = C.emask[:, 3:4]
                        pt, tp = PT[sb], tpt[sb]
                        C.act(pt, sbank[:, :], AF.Exp, [tsb, C.t_const], [tp], bias=bias)
                        last = (kv == 1 and c == 1)
                        C.mm(obank[:, :], Vw[:, rho * ncw + ut + c, kv, :], pt, first_o, last, [tp, tvw], [to])
                        first_o = False
                        for i in range(4):
                            C.mm(lbank[0:8, 0:128], oneh[:, 4 * kv + i, :], pt[:, i * 128:(i + 1) * 128],
                                 first_l, last and i == 3, [tp, tc], [tl])
                            first_l = False
                av = accO[:, :, qsl]
                C.tt("dve", av, obank[:, :].rearrange("p (i q) -> p i q", i=4), av, ALU.add, [to, taccO], [taccO])
                lv = accL[0:8, qsl]
                C.tt("dve", lv, lbank[0:8, 0:128], lv, ALU.add, [tl, taccL], [taccL])
    P.barrier()
    m.release(mk1)
    wo = v3(m.bf16(4 * D), 4)
    two = Tok("wo")
    wod = C.din("wo", [512, D]).rearrange("(kc p) n -> p kc n", p=128)
    load_w(C, wo, wod, two)
    fin = alloc_fin_tmp(C, 4)
    for b in range(NB):
        sl = slice(b * TB, (b + 1) * TB)
        emit_finalize_outproj(C, layer, b, 8, 4, lambda fc, sl=sl: accO[:, fc, sl], taccO, accL[0:8, sl], taccL,
                              sel, wo, two, fin)
    P.barrier()
    m.release(mk0)


def emit_ffn(C, layer):
    m = C.mem
    P = C.P
    mk0 = m.mark()
    hT, th = emit_h(C, layer, 1)
    wgu = [m.bf16(4 * 2 * KC * 128).rearrange("p (c s k j) -> p c s k j", c=4, s=2, k=KC) for _ in range(2)]
    wo = [v3(m.bf16(4 * D), 4) for _ in range(2)]
    tw = [Tok("ffw0"), Tok("ffw1")]
    actb = [v3(m.bf16(4 * TB), 4) for _ in range(2)]
    tact = [[Tok(f"act{i}_{c}") for c in range(4)] for i in range(2)]
    sg = [m.f32(TB), m.f32(TB)]
    tsg = [Tok("sg0"), Tok("sg1")]
    wind = C.din("w_in", [128, NFF, 2, KC, 128])
    woutd = C.din("w_out", [128, NFF, D])
    g2 = mod_vec(C, layer, 5)

    def load_group(gidx):
        c0, G = FF_GROUPS[gidx]
        i = gidx % 2
        load_w(C, wgu[i][:, 0:G], wind[:, c0:c0 + G], tw[i])
        load_w(C, wo[i][:, 0:G, :], woutd[:, c0:c0 + G, :], tw[i])

    load_group(0)
    cnt = 0
    ab = 0
    for gidx, (c0, G) in enumerate(FF_GROUPS):
        if gidx + 1 < len(FF_GROUPS):
            load_group(gidx + 1)
        i = gidx % 2
        for b in range(NB):
            sl = slice(b * TB, (b + 1) * TB)
            a = actb[ab % 2]
            ta = tact[ab % 2]
            ab += 1
            for c in range(G):
                gb, ub = 0 + (cnt % 2), 2 + (cnt % 2)
                s = cnt % 2
                cnt += 1
                for kc in range(KC):
                    C.mm(C.ps[gb][:, :], wgu[i][:, c, 0, kc, :], hT[:, kc, sl], kc == 0, kc == KC - 1,
                         [tw[i], th[kc][b]], [C.pst[gb]])
                for kc in range(KC):
                    C.mm(C.ps[ub][:, :], wgu[i][:, c, 1, kc, :], hT[:, kc, sl], kc == 0, kc == KC - 1,
                         [tw[i], th[kc][b]], [C.pst[ub]])
                C.act(sg[s], C.ps[gb][:, :], AF.Silu, [C.pst[gb]], [tsg[s]])
                C.tt("dve", a[:, c, :], sg[s], C.ps[ub][:, :], ALU.mult, [tsg[s], C.pst[ub]], [ta[c]])
            for mch in range(KC):
                yb = 4 + (mch % 4)
                for c in range(G):
                    C.mm(C.ps[yb][:, :], wo[i][:, c, mch * 128:(mch + 1) * 128], a[:, c, :], c == 0, c == G - 1,
                         [tw[i], ta[c]], [C.pst[yb]])
                xs = C.xT[:, mch, sl]
                C.stt(xs, C.ps[yb][:, :], g2[:, mch:mch + 1], xs, ALU.mult, ALU.add,
                      [C.pst[yb], C.tx[mch][b], C.t_const], [C.tx[mch][b]])
    P.barrier()
    m.release(mk0)


def emit_store_x(C):
    xd = C.dout("xT_out", [D, NT]).rearrange("(kc p) t -> p kc t", p=128)
    for kc in range(KC):
        for b in range(NB):
            C.dma("sp", xd[:, kc, b * TB:(b + 1) * TB], C.xT[:, kc, b * TB:(b + 1) * TB], [C.tx[kc][b]], [C.tx[kc][b]])
            C.out_toks.append(C.tx[kc][b])


def emit_final(C):
    m = C.mem
    od = C.dout("outT", [D, NT]).rearrange("(kc p) t -> p kc t", p=128)
    obuf = [m.f32(TB) for _ in range(4)]
    tob = [Tok(f"ob{i}") for i in range(4)]
    st = {"n": 0}

    def dst(kc, b):
        return obuf[st["n"] % 4]

    def dtok(kc, b):
        return tob[st["n"] % 4]

    def after(kc, b):
        i = st["n"] % 4
        C.dma("sp", od[:, kc, b * TB:(b + 1) * TB], obuf[i], [tob[i]], [tob[i]])
        st["n"] += 1

    emit_norm(C, C.gfin, None, dst, dtok, after_blk=after)
    C.out_toks += tob


def build_mod():
    nc = bass.Bass("TRN2", target_bir_lowering=False)
    C = Ctx(nc)
    m = C.mem
    cT = m.f32(8)
    cb = m.bf16(8)
    bsl = m.f32(24)
    res = m.f32(24)
    w = v3(m.bf16(KC * 3072), KC)
    tcn, tw, tr = Tok("c"), Tok("w"), Tok("res")
    C.dma("sp", cT, C.din("cT", [128, 8]), [], [tcn])
    C.dma("sp", bsl, C.din("bsl", [128, 24]), [], [tcn])
    wd = C.din("wsl", [D, 3072]).rearrange("(kc p) n -> p kc n", p=128)
    for kc in range(KC):
        load_w(C, w[:, kc, :], wd[:, kc, :], tw)
    C.act(cb, cT, AF.Silu, [tcn], [tcn])
    for j in range(24):
        for kc in range(KC):
            C.mm(C.ps[0][:, j:j + 1], w[:, kc, j * 128:(j + 1) * 128], cb[:, kc:kc + 1], kc == 0, kc == KC - 1,
                 [tw, tcn], [C.pst[0]])
    C.tt("dve", res, C.ps[0][:, 0:24], bsl, ALU.add, [C.pst[0], tcn], [tr])
    C.dma("sp", C.dout("modc", [128, 24]), res, [tr], [tr])
    C.P.final_wait("sp", [tr])
    C.P.emit()
    return nc


def build_prog(kind):
    nc = bass.Bass("TRN2", target_bir_lowering=False)
    C = Ctx(nc)
    setup_common(C)
    if kind == "kv0":
        C.kv_layer = 0
        emit_kv(C, "A")
    else:
        typ = "A" if kind == "mainA" else "B"
        if typ == "A":
            emit_attn_A(C, 0)
        else:
            emit_attn_B(C, 0)
        emit_ffn(C, 0)
        if kind == "mainBf":
            emit_final(C)
        else:
            C.kv_layer = 1
            emit_kv(C, "B" if typ == "A" else "A")
            emit_store_x(C)
    C.P.final_wait("sp", C.out_toks)
    C.P.emit()
    return nc


def _slopes(n):
    return (2.0 ** (-8.0 * np.arange(1, n + 1) / n)).astype(np.float32)


def _hilo(v):
    v = np.asarray(v, np.float32)
    hi = v.astype(NPBF)
    lo = (v - hi.astype(np.float32)).astype(NPBF)
    return hi, lo


def _consts_A():
    sl = _slopes(16)
    bmat = np.zeros((128, 4, 2, 512), NPBF)
    eye = np.eye(128, dtype=np.float32)
    for g in range(4):
        for i in range(4):
            hi, lo = _hilo(sl[4 * g + i])
            bmat[:, g, 0, i * 128:(i + 1) * 128] = (eye * np.float32(hi)).astype(NPBF)
            bmat[:, g, 1, i * 128:(i + 1) * 128] = (eye * np.float32(lo)).astype(NPBF)
    dmat = np.zeros((128, 3, 128), np.float32)
    q = np.arange(128)[:, None]
    j = np.arange(128)[None, :]
    for c in range(3):
        dist = np.abs(128 * (1 - c) + q - j)
        dmat[:, c, :] = np.where(dist <= 128, -dist, -32768.0)
    oneh = np.zeros((128, 16, 16), NPBF)
    for r in range(16):
        oneh[:, r, r] = 1.0
    sel = np.zeros((16, 8, 128), NPBF)
    for g in range(4):
        for i in range(4):
            fc = 4 * (g // 2) + i
            sel[4 * g + i, fc, (g % 2) * 64:(g % 2) * 64 + 64] = 1.0
    return bmat, dmat.astype(NPBF), oneh, sel


def _consts_B():
    sl = _slopes(24).reshape(3, 2, 4)
    bmat = np.zeros((128, 6, 2, 512), NPBF)
    eye = np.eye(128, dtype=np.float32)
    for gi in range(3):
        for kv in range(2):
            for i in range(4):
                hi, lo = _hilo(sl[gi, kv, i])
                bmat[:, 2 * gi + kv, 0, i * 128:(i + 1) * 128] = (eye * np.float32(hi)).astype(NPBF)
                bmat[:, 2 * gi + kv, 1, i * 128:(i + 1) * 128] = (eye * np.float32(lo)).astype(NPBF)
    dmat = np.zeros((128, 3, 2, 128), np.float32)
    q = np.arange(128)[:, None]
    j = np.arange(128)[None, :]
    for gi, d in enumerate(B_DIL):
        for c in range(2):
            rel = np.abs(q - j + 64 - 128 * c)
            dmat[:, gi, c, :] = np.where(rel <= 64, -(d * rel), -32768.0)
    oneh = np.zeros((128, 8, 8), NPBF)
    for r in range(8):
        oneh[:, r, r] = 1.0
    sel = np.zeros((8, 4, 128), NPBF)
    for kv in range(2):
        for i in range(4):
            sel[4 * kv + i, i, kv * 64:kv * 64 + 64] = 1.0
    return bmat, dmat.astype(NPBF), oneh, sel


def _fm(v):
    v = np.asarray(v, np.float32)
    return np.ascontiguousarray(v.reshape(-1, 128).T)


_PROGS = {}


def _prog(kind):
    if kind not in _PROGS:
        _PROGS[kind] = build_mod() if kind == "mod" else build_prog(kind)
    return _PROGS[kind]


def _run(kind, in_maps):
    res = run_bass_kernel_spmd(_prog(kind), in_maps, core_ids=list(range(NCORE)))
    return res.results


def _windows(parts, H, axis):
    full = np.concatenate(parts, axis=axis)
    pad = [(0, 0)] * full.ndim
    pad[axis] = (H, H)
    full = np.pad(full, pad)
    outs = []
    for r in range(NCORE):
        idx = [slice(None)] * full.ndim
        idx[axis] = slice(r * NT, r * NT + NT + 2 * H)
        outs.append(np.ascontiguousarray(full[tuple(idx)]))
    return outs


def kernel(x, c, ada_w, ada_b, norm_mix, norm_ffn, ffn_w_in, ffn_w_out, a_w_in, a_w_out, a_sink, b_w_in, b_w_out,
           final_norm, _debug=None):
    f = lambda a: np.asarray(a, dtype=np.float32)
    x, c, ada_w, ada_b = f(x), f(c), f(ada_w), f(ada_b)
    norm_mix, norm_ffn, ffn_w_in, ffn_w_out = f(norm_mix), f(norm_ffn), f(ffn_w_in), f(ffn_w_out)
    a_w_in, a_w_out, a_sink, b_w_in, b_w_out, final_norm = (f(a_w_in), f(a_w_out), f(a_sink), f(b_w_in),
                                                            f(b_w_out), f(final_norm))
    cT = _fm(c[0])
    maps = []
    for r in range(NCORE):
        i, h = r // 2, r % 2
        maps.append({"cT": cT, "wsl": np.ascontiguousarray(ada_w[i][:, h * 3072:(h + 1) * 3072]),
                     "bsl": _fm(ada_b[i][h * 3072:(h + 1) * 3072])})
    res = _run("mod", maps)
    modT = np.concatenate([np.asarray(res[r]["modc"], np.float32) for r in range(NCORE)], axis=1)
    gainT = np.concatenate([_fm(norm_mix[i]) for i in range(4)] + [_fm(norm_ffn[i]) for i in range(4)]
                           + [_fm(final_norm)], axis=1)

    def rot(i):
        order = [(i + k) % 4 for k in range(4)]
        mt = np.concatenate([modT[:, o * 48:(o + 1) * 48] for o in order], axis=1)
        gt = np.concatenate([gainT[:, o * 8:(o + 1) * 8] for o in order]
                            + [gainT[:, 32 + o * 8:32 + (o + 1) * 8] for o in order] + [gainT[:, 64:72]], axis=1)
        return np.ascontiguousarray(mt), np.ascontiguousarray(gt)

    emasks = []
    for r in range(NCORE):
        e = np.zeros((128, 4), np.float32)
        if r == 0:
            e[:, 0] = MASKV
            e[:64, 2] = MASKV
        if r == NCORE - 1:
            e[:, 1] = MASKV
            e[64:, 3] = MASKV
        emasks.append(e)
    xT = [np.ascontiguousarray(x[0, r * NT:(r + 1) * NT, :].T) for r in range(NCORE)]

    def a_q_perm():
        cols = []
        for hc in range(8):
            for half in range(2):
                g, i = 2 * (hc // 4) + half, hc % 4
                h = 4 * g + i
                cols.append(np.arange(h * 64, h * 64 + 64))
        return np.concatenate(cols)

    def b_q_perm():
        cols = []
        for gi in range(3):
            for i in range(4):
                for kv in range(2):
                    h = kv * 4 + i
                    cols.append(gi * 512 + np.arange(h * 64, h * 64 + 64))
        return np.concatenate(cols)

    def b_o_perm():
        rows = []
        for i in range(4):
            for kv in range(2):
                h = kv * 4 + i
                rows.append(np.arange(h * 64, h * 64 + 64))
        return np.concatenate(rows)

    aqp, bqp, bop = a_q_perm(), b_q_perm(), b_o_perm()
    cA, cB = _consts_A(), _consts_B()

    def kv_weights(i):
        j = i // 2
        if i % 2 == 0:
            return (np.ascontiguousarray(a_w_in[j][:, 1024:1280]), np.ascontiguousarray(a_w_in[j][:, 1280:1536]))
        return (np.ascontiguousarray(b_w_in[j][:, 1536:1920]), np.ascontiguousarray(b_w_in[j][:, 1920:2304]))

    mt, gt = rot(0)
    wk, wv = kv_weights(0)
    maps = [{"xT": xT[r], "modT": mt, "gainT": gt, "emask": emasks[r], "wk": wk, "wv": wv} for r in range(NCORE)]
    res = _run("kv0", maps)
    KT = [np.asarray(res[r]["KT"]) for r in range(NCORE)]
    Vp = [np.asarray(res[r]["Vp"]) for r in range(NCORE)]
    dbg = {}
    for i in range(4):
        j = i // 2
        mt, gt = rot(i)
        wi_t = np.ascontiguousarray(ffn_w_in[i].reshape(KC, 128, 2, NFF, 128).transpose(1, 3, 2, 0, 4))
        wo_t = np.ascontiguousarray(ffn_w_out[i].reshape(NFF, 128, D).transpose(1, 0, 2))
        base = {"modT": mt, "gainT": gt, "w_in": wi_t, "w_out": wo_t}
        if i % 2 == 0:
            kind = "mainA"
            KTw = _windows(KT, 128, 1)
            Vw = _windows(Vp, 128, 0)
            base.update({"wq": np.ascontiguousarray(a_w_in[j][:, :1024][:, aqp]),
                         "wo": np.ascontiguousarray(a_w_out[j][aqp, :]),
                         "sink": np.ascontiguousarray(a_sink[j].reshape(16, 1)),
                         "bmat": cA[0], "dmat": cA[1], "oneh": cA[2], "sel": cA[3]})
            per = [{"KTw": KTw[r], "Vw": Vw[r]} for r in range(NCORE)]
        else:
            kind = "mainB" if i < 3 else "mainBf"
            base.update({"wq": np.ascontiguousarray(b_w_in[j][:, :1536][:, bqp]),
                         "wo": np.ascontiguousarray(b_w_out[j][bop, :]),
                         "bmat": cB[0], "dmat": cB[1], "oneh": cB[2], "sel": cB[3]})
            per = [dict() for _ in range(NCORE)]
            for gi, d in enumerate(B_DIL):
                KTw = _windows([k[gi * 128:(gi + 1) * 128] for k in KT], 64 * d, 1)
                Vw = _windows([v[:, 2 * gi:2 * gi + 2] for v in Vp], 64 * d, 0)
                for r in range(NCORE):
                    per[r][f"KTw{gi}"] = KTw[r]
                    per[r][f"Vw{gi}"] = Vw[r]
        if i < 3:
            wk, wv = kv_weights(i + 1)
            base.update({"wk": wk, "wv": wv})
        maps = []
        for r in range(NCORE):
            mp = dict(base)
            mp.update(per[r])
            mp["xT"] = xT[r]
            mp["emask"] = emasks[r]
            maps.append(mp)
        res = _run(kind, maps)
        if i < 3:
            xT = [np.asarray(res[r]["xT_out"], np.float32) for r in range(NCORE)]
            KT = [np.asarray(res[r]["KT"]) for r in range(NCORE)]
            Vp = [np.asarray(res[r]["Vp"]) for r in range(NCORE)]
            if _debug is not None:
                _debug[f"x{i}"] = np.concatenate([t.T for t in xT], axis=0)
        else:
            out = np.concatenate([np.asarray(res[r]["outT"], np.float32).T for r in range(NCORE)], axis=0)
    return np.ascontiguousarray(out.reshape(1, SEQ, D).astype(np.float32))
```

```python
import contextlib
import numpy as np
import ml_dtypes
import concourse.bass as bass
import concourse.mybir as mybir
from concourse.bass_utils import run_bass_kernel_spmd

F32 = mybir.dt.float32
BF16 = mybir.dt.bfloat16
AF = mybir.ActivationFunctionType
ALU = mybir.AluOpType
NPBF = ml_dtypes.bfloat16

NCORE = 8
NT = 2048
SEQ = 16384
D = 1024
KC = 8
TB = 512
NB = 4
DFF = 2816
NFF = 22
EPS = 1e-6
MASKV = -1.0e4
FF_GROUPS = [(0, 4), (4, 4), (8, 4), (12, 4), (16, 3), (19, 3)]
B_DIL = (1, 4, 16)
ARENA_WORDS = 52000

ENGS = ("pe", "act", "dve", "pool", "sp")


class Tok:
    __slots__ = ("name", "writer", "readers")

    def __init__(self, name):
        self.name = name
        self.writer = None
        self.readers = []


class Prog:
    def __init__(self, nc, n_dma_sems=24):
        self.nc = nc
        self.q = {e: [] for e in ENGS}
        self.cnt = {e: 0 for e in ENGS}
        self.waited = {e: {} for e in ENGS}
        self.n_dma_sems = n_dma_sems
        self.dma_cnt = [0] * n_dma_sems
        self.dma_rr = 0
        self.sems = {}
        self.ncc = 0

    def _need(self, eng, deps):
        w = self.waited[eng]
        out = {}
        for d in deps:
            if d is None:
                continue
            k, v = d
            if w.get(k, 0) >= v:
                continue
            if out.get(k, 0) < v:
                out[k] = v
        for k, v in out.items():
            w[k] = v
        return list(out.items())

    def _deps(self, reads, writes):
        deps = []
        for t in reads:
            deps.append(t.writer)
        for t in writes:
            deps.append(t.writer)
            deps.extend(t.readers)
        return deps

    def _mark(self, me, reads, writes):
        for t in reads:
            t.readers.append(me)
        for t in writes:
            t.writer = me
            t.readers = []

    def op(self, eng, fn, reads=(), writes=(), signal=True):
        deps = self._deps(reads, writes)
        if eng == "pe":
            deps = [d for d in deps if d is not None and d[0] != "pe"]
        waits = self._need(eng, deps)
        if signal:
            self.cnt[eng] += 1
            me = (eng, self.cnt[eng])
            self.q[eng].append((waits, fn, (eng, 1)))
        else:
            me = (eng, self.cnt[eng] + 1)
            self.q[eng].append((waits, fn, "nosig"))
        self._mark(me, reads, writes)
        return me

    def dma(self, eng, fn, reads=(), writes=()):
        deps = self._deps(reads, writes)
        k = self.dma_rr
        self.dma_rr = (self.dma_rr + 1) % self.n_dma_sems
        key = ("dma", k)
        if self.dma_cnt[k] > 0:
            deps.append((key, self.dma_cnt[k]))
        waits = self._need(eng, deps)
        self.dma_cnt[k] += 16
        me = (key, self.dma_cnt[k])
        self.q[eng].append((waits, fn, (key, 16)))
        self._mark(me, reads, writes)
        return me

    def coll(self, fn, reads=(), writes=()):
        deps = self._deps(reads, writes)
        waits = self._need("pool", deps)
        key = ("cc", self.ncc)
        self.ncc += 1
        me = (key, 1)
        self.q["pool"].append((waits, fn, (key, None)))
        self._mark(me, reads, writes)
        return me

    def barrier(self):
        for e in ENGS:
            deps = [(x, self.cnt[x]) for x in ENGS if x != e and self.cnt[x] > 0]
            deps += [(("dma", k), self.dma_cnt[k]) for k in range(self.n_dma_sems) if self.dma_cnt[k] > 0]
            deps += [(("cc", k), 1) for k in range(self.ncc)]
            waits = self._need(e, deps)
            if waits:
                self.q[e].append((waits, None, None))

    def final_wait(self, eng, toks):
        deps = [t.writer for t in toks]
        waits = self._need(eng, deps)
        self.q[eng].append((waits, None, None))

    def emit(self):
        nc = self.nc
        with contextlib.ExitStack() as st:
            for e in ENGS:
                self.sems[e] = st.enter_context(nc.semaphore(f"s_{e}"))
            for k in range(self.n_dma_sems):
                self.sems[("dma", k)] = st.enter_context(nc.semaphore(f"s_dma{k}"))
            for k in range(self.ncc):
                self.sems[("cc", k)] = st.enter_context(nc.semaphore(f"s_cc{k}"))
            block = st.enter_context(nc.Block())

            def run(engname):
                def body(engine):
                    for waits, fn, inc in self.q[engname]:
                        for k, v in waits:
                            engine.wait_ge(self.sems[k], v)
                        if fn is not None:
                            ins = fn(engine)
                            if inc == "nosig":
                                continue
                            if inc[1] is None:
                                ins.then_inc(self.sems[inc[0]])
                            else:
                                ins.then_inc(self.sems[inc[0]], inc[1])
                return body

            block.tensor(run("pe"))
            block.scalar(run("act"))
            block.vector(run("dve"))
            block.gpsimd(run("pool"))
            block.sync(run("sp"))


class Mem:
    def __init__(self, nc, words):
        self.arena = nc.alloc_sbuf_tensor("arena", [128, words], F32)
        self.words = words
        self.top = 0

    def mark(self):
        return self.top

    def release(self, m):
        self.top = m

    def f32(self, n):
        off = self.top
        self.top += (n + 7) // 8 * 8
        assert self.top <= self.words, f"SBUF arena overflow {self.top} > {self.words}"
        return self.arena[:, off:off + n]

    def bf16(self, n):
        w = (n + 1) // 2
        off = self.top
        self.top += (w + 7) // 8 * 8
        assert self.top <= self.words, f"SBUF arena overflow {self.top} > {self.words}"
        return self.arena[:, off:off + w].bitcast(BF16)[:, 0:n]


class Ctx:
    def __init__(self, nc):
        self.nc = nc
        self.P = Prog(nc)
        self.mem = Mem(nc, ARENA_WORDS)
        self.ps = [nc.alloc_psum_tensor(f"psb{b}", [128, 512], F32) for b in range(8)]
        self.pst = [Tok(f"ps{b}") for b in range(8)]
        self.dram = {}
        self.out_toks = []

    def din(self, name, shape, dt=F32):
        t = self.nc.dram_tensor(name, list(shape), dt, kind="ExternalInput").ap()
        self.dram[name] = t
        return t

    def dout(self, name, shape, dt=F32):
        t = self.nc.dram_tensor(name, list(shape), dt, kind="ExternalOutput").ap()
        self.dram[name] = t
        return t

    def mm(self, out, lhsT, rhs, start, stop, reads, writes):
        self.P.op("pe", lambda e: e.matmul(out, lhsT=lhsT, rhs=rhs, start=start, stop=stop), reads, writes,
                  signal=bool(stop))

    def act(self, out, in_, func, reads, writes, bias=None, scale=None):
        kw = {}
        if bias is not None:
            kw["bias"] = bias
        if scale is not None:
            kw["scale"] = scale
        self.P.op("act", lambda e: e.activation(out=out, in_=in_, func=func, **kw), reads, writes)

    def tt(self, eng, out, in0, in1, op, reads, writes):
        self.P.op(eng, lambda e: e.tensor_tensor(out=out, in0=in0, in1=in1, op=op), reads, writes)

    def ts(self, eng, out, in0, s1, s2, op0, op1, reads, writes):
        if op1 is None:
            self.P.op(eng, lambda e: e.tensor_scalar(out=out, in0=in0, scalar1=s1, scalar2=None, op0=op0), reads, writes)
        else:
            self.P.op(eng, lambda e: e.tensor_scalar(out=out, in0=in0, scalar1=s1, scalar2=s2, op0=op0, op1=op1),
                      reads, writes)

    def stt(self, out, in0, scalar, in1, op0, op1, reads, writes):
        self.P.op("dve", lambda e: e.scalar_tensor_tensor(out=out, in0=in0, scalar=scalar, in1=in1, op0=op0, op1=op1),
                  reads, writes)

    def copy(self, eng, out, in_, reads, writes):
        if eng == "act":
            self.P.op("act", lambda e: e.activation(out=out, in_=in_, func=AF.Copy), reads, writes)
        else:
            self.P.op(eng, lambda e: e.tensor_copy(out=out, in_=in_), reads, writes)

    def memset(self, eng, ap, val, writes):
        self.P.op(eng, lambda e: e.memset(ap, val), (), writes)

    def dma(self, q, out, in_, reads, writes):
        self.P.dma(q, lambda e: e.dma_start(out=out, in_=in_), reads, writes)


def v3(ap, a):
    return ap.rearrange("p (a b) -> p a b", a=a)


def v4(ap, a, b):
    return ap.rearrange("p (a b c) -> p a b c", a=a, b=b)


def setup_common(C, need_x=True):
    m = C.mem
    if need_x:
        C.xT = v3(m.f32(KC * NT), KC)
        C.tx = [[Tok(f"x{kc}_{b}") for b in range(NB)] for kc in range(KC)]
        xd = C.din("xT", [D, NT]).rearrange("(kc p) t -> p kc t", p=128)
        for kc in range(KC):
            for b in range(NB):
                C.dma("sp", C.xT[:, kc, b * TB:(b + 1) * TB], xd[:, kc, b * TB:(b + 1) * TB], [], [C.tx[kc][b]])
    C.mod = m.f32(192)
    C.gain = m.f32(72)
    C.emask = m.f32(4)
    C.gsc = m.f32(64)
    C.gfin = m.f32(8)
    C.ones = m.bf16(128)
    C.t_const = Tok("const")
    tmp = m.f32(8)
    C.dma("sp", C.mod, C.din("modT", [128, 192]), [], [C.t_const])
    C.dma("sp", C.gain, C.din("gainT", [128, 72]), [], [C.t_const])
    C.dma("sp", C.emask, C.din("emask", [128, 4]), [], [C.t_const])
    C.memset("pool", C.ones, 1.0, [C.t_const])
    C.epsb = m.f32(1)
    C.memset("pool", C.epsb, 1024.0 * EPS, [C.t_const])
    for i in range(4):
        for w in range(2):
            sc = C.mod[:, i * 48 + (3 * w + 1) * 8: i * 48 + (3 * w + 1) * 8 + 8]
            g = C.gain[:, (w * 4 + i) * 8:(w * 4 + i) * 8 + 8]
            o = C.gsc[:, (i * 2 + w) * 8:(i * 2 + w) * 8 + 8]
            C.ts("dve", tmp, sc, 1.0, 32.0, ALU.add, ALU.mult, [C.t_const], [C.t_const])
            C.tt("dve", o, tmp, g, ALU.mult, [C.t_const], [C.t_const])
    C.ts("dve", C.gfin, C.gain[:, 64:72], 32.0, None, ALU.mult, None, [C.t_const], [C.t_const])


def mod_vec(C, i, v):
    return C.mod[:, i * 48 + v * 8: i * 48 + v * 8 + 8]


def emit_norm(C, gsc, sh, dst_fn, dst_tok_fn, after_blk=None):
    m = C.mem
    sq = [v3(m.bf16(KC * TB), KC) for _ in range(2)]
    tsq = [[Tok(f"sq{i}_{kc}") for kc in range(KC)] for i in range(2)]
    rstd = [m.f32(TB) for _ in range(2)]
    trs = [Tok("rstd0"), Tok("rstd1")]
    y = [m.f32(TB) for _ in range(2)]
    ty = [Tok("y0"), Tok("y1")]
    bank, tb = C.ps[7], C.pst[7]
    for b in range(NB):
        s = sq[b % 2]
        ts_ = tsq[b % 2]
        sl = slice(b * TB, (b + 1) * TB)
        for kc in range(KC):
            C.act(s[:, kc, :], C.xT[:, kc, sl], AF.Square, [C.tx[kc][b]], [ts_[kc]])
        for kc in range(KC):
            C.mm(bank[:, :], C.ones, s[:, kc, :], kc == 0, kc == KC - 1, [ts_[kc], C.t_const], [tb])
        r = rstd[b % 2]
        C.act(r, bank[:, :], AF.Ln, [tb, C.t_const], [trs[b % 2]], bias=C.epsb[:, 0:1])
        C.act(r, r, AF.Exp, [trs[b % 2]], [trs[b % 2]], scale=-0.5)
        for kc in range(KC):
            yy = y[kc % 2]
            C.tt("dve", yy, C.xT[:, kc, sl], r, ALU.mult, [C.tx[kc][b], trs[b % 2]], [ty[kc % 2]])
            if sh is not None:
                C.act(dst_fn(kc, b), yy, AF.Identity, [ty[kc % 2], C.t_const], [dst_tok_fn(kc, b)],
                      bias=sh[:, kc:kc + 1], scale=gsc[:, kc:kc + 1])
            else:
                C.act(dst_fn(kc, b), yy, AF.Identity, [ty[kc % 2], C.t_const], [dst_tok_fn(kc, b)],
                      scale=gsc[:, kc:kc + 1])
            if after_blk is not None:
                after_blk(kc, b)


def load_w(C, dst, src, tok):
    C.dma("pool", dst, src, [], [tok])


def emit_h(C, layer, which):
    hT = v3(C.mem.bf16(KC * NT), KC)
    th = [[Tok(f"h{kc}_{b}") for b in range(NB)] for kc in range(KC)]
    gsc = C.gsc[:, (layer * 2 + which) * 8:(layer * 2 + which) * 8 + 8]
    sh = mod_vec(C, layer, 3 * which)
    emit_norm(C, gsc, sh, lambda kc, b: hT[:, kc, b * TB:(b + 1) * TB], lambda kc, b: th[kc][b])
    return hT, th


def emit_kv(C, typ):
    m = C.mem
    mk = m.mark()
    CK = 256 if typ == "A" else 384
    nch, nkv = CK // 128, CK // 64
    layer = C.kv_layer
    wk = v3(m.bf16(KC * CK), KC)
    wv = v3(m.bf16(KC * CK), KC)
    twk, twv = Tok("wk"), Tok("wv")
    wkd = C.din("wk", [D, CK]).rearrange("(kc p) n -> p kc n", p=128)
    wvd = C.din("wv", [D, CK]).rearrange("(kc p) n -> p kc n", p=128)
    load_w(C, wk, wkd, twk)
    load_w(C, wv, wvd, twv)
    kT = v3(m.bf16(nch * NT), nch)
    tk = [Tok(f"kT{c}") for c in range(nch)]
    Vp = m.bf16(16 * nkv * 128)
    tv = Tok("Vp")
    C.memset("pool", Vp, 0.0, [tv])
    Vp5 = Vp.rearrange("p (t k two d) -> p t k two d", t=16, k=nkv // 2, two=2)
    hT, th = emit_h(C, layer, 0)
    KTd = C.dout("KT", [CK, NT], BF16)
    Vpd = C.dout("Vp", [NT, nkv, 128], BF16)
    cnt = 0
    for b in range(NB):
        sl = slice(b * TB, (b + 1) * TB)
        for ch in range(nch):
            bk = 5 + (cnt % 2)
            cnt += 1
            for kc in range(KC):
                C.mm(C.ps[bk][:, :], wk[:, kc, ch * 128:(ch + 1) * 128], hT[:, kc, sl], kc == 0, kc == KC - 1,
                     [twk, th[kc][b]], [C.pst[bk]])
            C.copy("act", kT[:, ch, sl], C.ps[bk][:, :], [C.pst[bk]], [tk[ch]])
    for ch in range(nch):
        C.dma("sp", KTd[ch * 128:(ch + 1) * 128, :], kT[:, ch, :], [tk[ch]], [tk[ch]])
    for tt_ in range(16):
        b = tt_ // 4
        bk = 5 + (cnt % 2)
        cnt += 1
        for kc in range(KC):
            C.mm(C.ps[bk][:, 0:CK], hT[:, kc, tt_ * 128:(tt_ + 1) * 128], wv[:, kc, :], kc == 0, kc == KC - 1,
                 [twv, th[kc][b]], [C.pst[bk]])
        pv = C.ps[bk][:, 0:CK].rearrange("p (k two d) -> p k two d", k=nkv // 2, two=2)
        C.copy("dve", Vp5[:, tt_, :, 0, 0:64], pv[:, :, 0, :], [C.pst[bk]], [tv])
        C.copy("act", Vp5[:, tt_, :, 1, 64:128], pv[:, :, 1, :], [C.pst[bk]], [tv])
    C.dma("sp", Vpd.rearrange("(t p) k d -> p t k d", p=128), Vp.rearrange("p (t k d) -> p t k d", t=16, k=nkv),
          [tv], [tv])
    C.out_toks += tk + [tv]
    C.P.barrier()
    m.release(mk)


def emit_q(C, layer, wq_name, ncol):
    m = C.mem
    nq = ncol // 128
    qT = v3(m.bf16(nq * NT), nq)
    tq = [[Tok(f"q{c}_{b}") for b in range(NB)] for c in range(nq)]
    mk = m.mark()
    wq = v3(m.bf16(KC * ncol), KC)
    twq = Tok("wq")
    wqd = C.din(wq_name, [D, ncol]).rearrange("(kc p) n -> p kc n", p=128)
    for kc in range(KC):
        load_w(C, wq[:, kc, :], wqd[:, kc, :], twq)
    hT, th = emit_h(C, layer, 0)
    cnt = 0
    for b in range(NB):
        sl = slice(b * TB, (b + 1) * TB)
        for c in range(nq):
            bk = 5 + (cnt % 2)
            cnt += 1
            for kc in range(KC):
                C.mm(C.ps[bk][:, :], wq[:, kc, c * 128:(c + 1) * 128], hT[:, kc, sl], kc == 0, kc == KC - 1,
                     [twq, th[kc][b]], [C.pst[bk]])
            C.act(qT[:, c, sl], C.ps[bk][:, :], AF.Copy, [C.pst[bk]], [tq[c][b]], scale=0.125)
    C.P.barrier()
    m.release(mk)
    return qT, tq


def emit_finalize_outproj(C, layer, b, R, nfc, accO_fn, taccO, accL_ap, taccL, sel, wo, two, tmp):
    rl, rh, rlo, rlbc, OTn, t_rl, t_rlbc, t_otn = tmp
    sl = slice(b * TB, (b + 1) * TB)
    C.P.op("dve", lambda e: e.reciprocal(out=rl[0:R, :], in_=accL_ap), [taccL], [t_rl])
    C.copy("dve", rh[0:R, :], rl[0:R, :], [t_rl], [t_rl])
    C.tt("dve", rlo[0:R, :], rl[0:R, :], rh[0:R, :], ALU.subtract, [t_rl], [t_rl])
    for fc in range(nfc):
        bk = 5 + (fc % 2)
        C.mm(C.ps[bk][:, :], sel[0:R, fc, :], rh[0:R, :], True, False, [t_rl, C.t_const], [C.pst[bk]])
        C.mm(C.ps[bk][:, :], sel[0:R, fc, :], rlo[0:R, :], False, True, [t_rl, C.t_const], [C.pst[bk]])
        C.copy("act", rlbc[fc % 2], C.ps[bk][:, :], [C.pst[bk]], [t_rlbc[fc % 2]])
        C.tt("dve", OTn[:, fc, :], accO_fn(fc), rlbc[fc % 2], ALU.mult, [taccO, t_rlbc[fc % 2]], [t_otn[fc]])
    g1 = mod_vec(C, layer, 2)
    for mch in range(KC):
        bk = 5 + (mch % 2)
        for fc in range(nfc):
            C.mm(C.ps[bk][:, :], wo[:, fc, mch * 128:(mch + 1) * 128], OTn[:, fc, :], fc == 0, fc == nfc - 1,
                 [two, t_otn[fc]], [C.pst[bk]])
        xs = C.xT[:, mch, sl]
        C.stt(xs, C.ps[bk][:, :], g1[:, mch:mch + 1], xs, ALU.mult, ALU.add,
              [C.pst[bk], C.tx[mch][b], C.t_const], [C.tx[mch][b]])


def alloc_fin_tmp(C, nfc):
    m = C.mem
    rl = m.f32(TB)
    rh = m.bf16(TB)
    rlo = m.bf16(TB)
    rlbc = [m.f32(TB), m.f32(TB)]
    OTn = v3(m.bf16(nfc * TB), nfc)
    return (rl, rh, rlo, rlbc, OTn, Tok("rl"), [Tok("rlbc0"), Tok("rlbc1")], [Tok(f"otn{f}") for f in range(nfc)])


def emit_attn_A(C, layer):
    m = C.mem
    P = C.P
    mk0 = m.mark()
    qT, tq = emit_q(C, layer, "wq", 1024)
    bmat = v4(m.bf16(4 * 2 * 512), 4, 2)
    dmat = v3(m.bf16(3 * 128), 3)
    oneh = v3(m.bf16(16 * 16), 16)
    sel = v3(m.bf16(8 * 128), 8)
    esink = m.f32(1)
    tc = Tok("attc")
    C.dma("sp", bmat, C.din("bmat", [128, 4, 2, 512], BF16), [], [tc])
    C.dma("sp", dmat, C.din("dmat", [128, 3, 128], BF16), [], [tc])
    C.dma("sp", oneh, C.din("oneh", [128, 16, 16], BF16), [], [tc])
    C.dma("sp", sel[0:16], C.din("sel", [16, 8, 128], BF16), [], [tc])
    C.dma("sp", esink[0:16, :], C.din("sink", [16, 1]), [], [tc])
    C.act(esink[0:16, :], esink[0:16, :], AF.Exp, [tc], [tc])
    wo = v3(m.bf16(KC * D), KC)
    two = Tok("wo")
    wod = C.din("wo", [D, D]).rearrange("(kc p) n -> p kc n", p=128)
    for kc in range(KC):
        load_w(C, wo[:, kc, :], wod[:, kc, :], two)
    kTw = [v3(m.bf16(2 * 768), 2) for _ in range(2)]
    Vw = [v4(m.bf16(6 * 4 * 128), 6, 4) for _ in range(2)]
    tkw = [Tok("kTw0"), Tok("kTw1")]
    tvw = [Tok("Vw0"), Tok("Vw1")]
    accO = v3(m.f32(8 * TB), 8)
    taccO = Tok("accO")
    accL = m.f32(TB)
    taccL = Tok("accL")
    PT = [m.bf16(512), m.bf16(512)]
    tpt = [Tok("pt0"), Tok("pt1")]
    fin = alloc_fin_tmp(C, 8)
    KTd = C.din("KTw", [256, NT + 256], BF16)
    Vd = C.din("Vw", [NT + 256, 4, 128], BF16).rearrange("(t p) k d -> p t k d", p=128)
    scnt = 0
    pend = []

    def flush():
        while pend:
            pend.pop(0)()
    for b in range(NB):
        kb, vb = kTw[b % 2], Vw[b % 2]
        for ch in range(2):
            C.dma("sp", kb[:, ch, :], KTd[ch * 128:(ch + 1) * 128, b * TB:b * TB + 768], [], [tkw[b % 2]])
        C.dma("sp", vb, Vd[:, 4 * b:4 * b + 6], [], [tvw[b % 2]])
        for qi in range(4):
            j = 4 * b + qi
            qs = slice(b * TB + qi * 128, b * TB + (qi + 1) * 128)
            lbank, tl = C.ps[4], C.pst[4]
            first_l = True
            for pair in range(2):
                obank, to = C.ps[2 + pair], C.pst[2 + pair]
                first_o = True
                for g in (2 * pair, 2 * pair + 1):
                    half = g % 2
                    hs = slice(half * 64, half * 64 + 64)
                    for c in range(3):
                        sb = scnt % 2
                        scnt += 1
                        sbank, tsb = C.ps[sb], C.pst[sb]
                        C.mm(sbank[:, :], dmat[:, c, :], bmat[:, g, 0, :], True, False, [tc], [tsb])
                        C.mm(sbank[:, :], dmat[:, c, :], bmat[:, g, 1, :], False, False, [tc], [tsb])
                        C.mm(sbank[:, :], kb[hs, pair, (qi + c) * 128:(qi + c + 1) * 128],
                             qT[hs, 4 * pair:4 * pair + 4, qs], False, True,
                             [tkw[b % 2]] + [tq[4 * pair + i][b] for i in range(4)], [tsb])
                        bias = None
                        if j == 0 and c == 0:
                            bias = C.emask[:, 0:1]
                        if j == 15 and c == 2:
                            bias = C.emask[:, 1:2]
                        pt, tp = PT[sb], tpt[sb]
                        C.act(pt, sbank[:, :], AF.Exp, [tsb, C.t_const], [tp], bias=bias)
                        last_o = (g == 2 * pair + 1 and c == 2)
                        flush()

                        def pv(pt=pt, tp=tp, obank=obank, to=to, lbank=lbank, tl=tl, g=g, c=c, qi=qi, pair=pair,
                               first_o=first_o, last_o=last_o, first_l=first_l, vb=vb, b=b):
                            C.mm(obank[:, :], vb[:, qi + c, g, :], pt, first_o, last_o, [tp, tvw[b % 2]], [to])
                            for i in range(4):
                                last_l = (pair == 1 and last_o and i == 3)
                                C.mm(lbank[0:16, 0:128], oneh[:, 4 * g + i, :], pt[:, i * 128:(i + 1) * 128],
                                     first_l and i == 0, last_l, [tp, tc], [tl])
                            if last_o:
                                C.copy("act", accO[:, 4 * pair:4 * pair + 4, qi * 128:(qi + 1) * 128],
                                       obank[:, :].rearrange("p (i q) -> p i q", i=4), [to], [taccO])
                                if pair == 1:
                                    C.ts("dve", accL[0:16, qi * 128:(qi + 1) * 128], lbank[0:16, 0:128],
                                         esink[0:16, 0:1], None, ALU.add, None, [tl, tc], [taccL])
                        pend.append(pv)
                        first_o = False
                        first_l = False
        flush()
        emit_finalize_outproj(C, layer, b, 16, 8, lambda fc: accO[:, fc, :], taccO, accL[0:16, :], taccL,
                              sel, wo, two, fin)
    P.barrier()
    m.release(mk0)


def emit_attn_B(C, layer):
    m = C.mem
    P = C.P
    mk0 = m.mark()
    qT, tq = emit_q(C, layer, "wq", 1536)
    accO = v3(m.f32(4 * NT), 4)
    accL = m.f32(NT)
    taccO, taccL = Tok("accO"), Tok("accL")
    C.memset("pool", accO.rearrange("p a b -> p (a b)"), 0.0, [taccO])
    C.memset("pool", accL, 0.0, [taccL])
    oneh = v3(m.bf16(8 * 8), 8)
    sel = v3(m.bf16(4 * 128), 4)
    tc = Tok("attc")
    C.dma("sp", oneh, C.din("oneh", [128, 8, 8], BF16), [], [tc])
    C.dma("sp", sel[0:8], C.din("sel", [8, 4, 128], BF16), [], [tc])
    mk1 = m.mark()
    bmat = v4(m.bf16(6 * 2 * 512), 6, 2)
    dmat = v4(m.bf16(3 * 2 * 128), 3, 2)
    C.dma("sp", bmat, C.din("bmat", [128, 6, 2, 512], BF16), [], [tc])
    C.dma("sp", dmat, C.din("dmat", [128, 3, 2, 128], BF16), [], [tc])
    PT = [m.bf16(512), m.bf16(512)]
    tpt = [Tok("pt0"), Tok("pt1")]
    WMAX = NT + 128 * 16
    kTw_buf = m.bf16(WMAX)
    Vw_buf = m.bf16(32 * 2 * 128)
    tkw, tvw = Tok("kTw"), Tok("Vw")
    scnt = 0
    ocnt = 0
    pendB = []

    def flushB():
        while pendB:
            pendB.pop(0)()

    for gi, d in enumerate(B_DIL):
        W = NT + 128 * d
        ncw = 16 // d + 1
        U = NT // d
        nut = U // 128
        kTw = kTw_buf[:, 0:W]
        Vw = v4(Vw_buf[:, 0:d * ncw * 256], d * ncw, 2)
        C.dma("sp", kTw, C.din(f"KTw{gi}", [128, W], BF16), [], [tkw])
        Vd = C.din(f"Vw{gi}", [W, 2, 128], BF16).rearrange("(w dd) k e -> dd w k e", dd=d)
        for rho in range(d):
            C.dma("sp", Vw[:, rho * ncw:(rho + 1) * ncw], Vd[rho].rearrange("(cw p) k e -> p cw k e", p=128),
                  [], [tvw])
        for rho in range(d):
            for ut in range(nut):
                t0 = rho + d * 128 * ut
                qsl = slice(t0, t0 + 127 * d + 1, d)
                blks = sorted(set([t0 // TB, (t0 + d * 127) // TB]))
                blks = list(range(blks[0], blks[-1] + 1))
                ob = 2 + (ocnt % 2)
                lb = 4 + (ocnt % 2)
                ocnt += 1
                obank, to = C.ps[ob], C.pst[ob]
                lbank, tl = C.ps[lb], C.pst[lb]
                first_o, first_l = True, True
                for kv in range(2):
                    hs = slice(kv * 64, kv * 64 + 64)
                    for c in range(2):
                        sb = scnt % 2
                        scnt += 1
                        sbank, tsb = C.ps[sb], C.pst[sb]
                        k0 = rho + d * 128 * (ut + c)
                        C.mm(sbank[:, :], dmat[:, gi, c, :], bmat[:, 2 * gi + kv, 0, :], True, False, [tc], [tsb])
                        C.mm(sbank[:, :], dmat[:, gi, c, :], bmat[:, 2 * gi + kv, 1, :], False, False, [tc], [tsb])
                        C.mm(sbank[:, :], kTw[hs, k0:k0 + 127 * d + 1:d], qT[hs, 4 * gi:4 * gi + 4, qsl], False, True,
                             [tkw] + [tq[4 * gi + i][bb] for i in range(4) for bb in blks], [tsb])
                        bias = None
                        if ut == 0 and c == 0:
                            bias = C.emask[:, 2:3]
                        if ut == nut - 1 and c == 1:
                            bias = C.emask[:, 3:4]
                        pt, tp = PT[sb], tpt[sb]
                        C.act(pt, sbank[:, :], AF.Exp, [tsb, C.t_const], [tp], bias=bias)
                        last = (kv == 1 and c == 1)
                        flushB()

                        def pv(pt=pt, tp=tp, obank=obank, to=to, lbank=lbank, tl=tl, kv=kv, c=c, qsl=qsl,
                               first_o=first_o, first_l=first_l, last=last, vidx=rho * ncw + ut + c, Vw=Vw):
                            C.mm(obank[:, :], Vw[:, vidx, kv, :], pt, first_o, last, [tp, tvw], [to])
                            for i in range(4):
                                C.mm(lbank[0:8, 0:128], oneh[:, 4 * kv + i, :], pt[:, i * 128:(i + 1) * 128],
                                     first_l and i == 0, last and i == 3, [tp, tc], [tl])
                            if last:
                                av = accO[:, :, qsl]
                                C.tt("dve", av, obank[:, :].rearrange("p (i q) -> p i q", i=4), av, ALU.add,
                                     [to, taccO], [taccO])
                                lv = accL[0:8, qsl]
                                C.tt("dve", lv, lbank[0:8, 0:128], lv, ALU.add, [tl, taccL], [taccL])
                        pendB.append(pv)
                        first_o = False
                        first_l = False
        flushB()
    P.barrier()
    m.release(mk1)
    wo = v3(m.bf16(4 * D), 4)
    two = Tok("wo")
    wod = C.din("wo", [512, D]).rearrange("(kc p) n -> p kc n", p=128)
    load_w(C, wo, wod, two)
    fin = alloc_fin_tmp(C, 4)
    for b in range(NB):
        sl = slice(b * TB, (b + 1) * TB)
        emit_finalize_outproj(C, layer, b, 8, 4, lambda fc, sl=sl: accO[:, fc, sl], taccO, accL[0:8, sl], taccL,
                              sel, wo, two, fin)
    P.barrier()
    m.release(mk0)


def emit_ffn(C, layer):
    m = C.mem
    P = C.P
    mk0 = m.mark()
    hT, th = emit_h(C, layer, 1)
    wgu = [m.bf16(4 * 2 * KC * 128).rearrange("p (c s k j) -> p c s k j", c=4, s=2, k=KC) for _ in range(2)]
    wo = [v3(m.bf16(4 * D), 4) for _ in range(2)]
    tw = [Tok("ffw0"), Tok("ffw1")]
    actb = [v3(m.bf16(4 * TB), 4) for _ in range(2)]
    tact = [[Tok(f"act{i}_{c}") for c in range(4)] for i in range(2)]
    sg = [m.f32(TB), m.f32(TB)]
    tsg = [Tok("sg0"), Tok("sg1")]
    wind = C.din("w_in", [128, NFF, 2, KC, 128])
    woutd = C.din("w_out", [128, NFF, D])
    g2 = mod_vec(C, layer, 5)

    def load_group(gidx):
        c0, G = FF_GROUPS[gidx]
        i = gidx % 2
        load_w(C, wgu[i][:, 0:G], wind[:, c0:c0 + G], tw[i])
        load_w(C, wo[i][:, 0:G, :], woutd[:, c0:c0 + G, :], tw[i])

    load_group(0)
    cnt = 0
    ab = 0
    for gidx, (c0, G) in enumerate(FF_GROUPS):
        if gidx + 1 < len(FF_GROUPS):
            load_group(gidx + 1)
        i = gidx % 2
        for b in range(NB):
            sl = slice(b * TB, (b + 1) * TB)
            a = actb[ab % 2]
            ta = tact[ab % 2]
            ab += 1
            for c in range(G):
                gb, ub = 0 + (cnt % 2), 2 + (cnt % 2)
                s = cnt % 2
                cnt += 1
                for kc in range(KC):
                    C.mm(C.ps[gb][:, :], wgu[i][:, c, 0, kc, :], hT[:, kc, sl], kc == 0, kc == KC - 1,
                         [tw[i], th[kc][b]], [C.pst[gb]])
                for kc in range(KC):
                    C.mm(C.ps[ub][:, :], wgu[i][:, c, 1, kc, :], hT[:, kc, sl], kc == 0, kc == KC - 1,
                         [tw[i], th[kc][b]], [C.pst[ub]])
                C.act(sg[s], C.ps[gb][:, :], AF.Silu, [C.pst[gb]], [tsg[s]])
                C.tt("dve", a[:, c, :], sg[s], C.ps[ub][:, :], ALU.mult, [tsg[s], C.pst[ub]], [ta[c]])
            for mch in range(KC):
                yb = 4 + (mch % 4)
                for c in range(G):
                    C.mm(C.ps[yb][:, :], wo[i][:, c, mch * 128:(mch + 1) * 128], a[:, c, :], c == 0, c == G - 1,
                         [tw[i], ta[c]], [C.pst[yb]])
                xs = C.xT[:, mch, sl]
                C.stt(xs, C.ps[yb][:, :], g2[:, mch:mch + 1], xs, ALU.mult, ALU.add,
                      [C.pst[yb], C.tx[mch][b], C.t_const], [C.tx[mch][b]])
    P.barrier()
    m.release(mk0)


def emit_store_x(C):
    xd = C.dout("xT_out", [D, NT]).rearrange("(kc p) t -> p kc t", p=128)
    for kc in range(KC):
        for b in range(NB):
            C.dma("sp", xd[:, kc, b * TB:(b + 1) * TB], C.xT[:, kc, b * TB:(b + 1) * TB], [C.tx[kc][b]], [C.tx[kc][b]])
            C.out_toks.append(C.tx[kc][b])


def emit_final(C):
    m = C.mem
    od = C.dout("outT", [D, NT]).rearrange("(kc p) t -> p kc t", p=128)
    obuf = [m.f32(TB) for _ in range(4)]
    tob = [Tok(f"ob{i}") for i in range(4)]
    st = {"n": 0}

    def dst(kc, b):
        return obuf[st["n"] % 4]

    def dtok(kc, b):
        return tob[st["n"] % 4]

    def after(kc, b):
        i = st["n"] % 4
        C.dma("sp", od[:, kc, b * TB:(b + 1) * TB], obuf[i], [tob[i]], [tob[i]])
        st["n"] += 1

    emit_norm(C, C.gfin, None, dst, dtok, after_blk=after)
    C.out_toks += tob


def build_mod():
    nc = bass.Bass("TRN2", target_bir_lowering=False)
    C = Ctx(nc)
    m = C.mem
    cT = m.f32(8)
    cb = m.bf16(8)
    bsl = m.f32(24)
    res = m.f32(24)
    w = v3(m.bf16(KC * 3072), KC)
    tcn, tw, tr = Tok("c"), Tok("w"), Tok("res")
    C.dma("sp", cT, C.din("cT", [128, 8]), [], [tcn])
    C.dma("sp", bsl, C.din("bsl", [128, 24]), [], [tcn])
    wd = C.din("wsl", [D, 3072]).rearrange("(kc p) n -> p kc n", p=128)
    for kc in range(KC):
        load_w(C, w[:, kc, :], wd[:, kc, :], tw)
    C.act(cb, cT, AF.Silu, [tcn], [tcn])
    for j in range(24):
        for kc in range(KC):
            C.mm(C.ps[0][:, j:j + 1], w[:, kc, j * 128:(j + 1) * 128], cb[:, kc:kc + 1], kc == 0, kc == KC - 1,
                 [tw, tcn], [C.pst[0]])
    C.tt("dve", res, C.ps[0][:, 0:24], bsl, ALU.add, [C.pst[0], tcn], [tr])
    C.dma("sp", C.dout("modc", [128, 24]), res, [tr], [tr])
    C.P.final_wait("sp", [tr])
    C.P.emit()
    return nc


def build_prog(kind):
    nc = bass.Bass("TRN2", target_bir_lowering=False)
    C = Ctx(nc)
    setup_common(C)
    if kind == "kv0":
        C.kv_layer = 0
        emit_kv(C, "A")
    else:
        typ = "A" if kind == "mainA" else "B"
        if typ == "A":
            emit_attn_A(C, 0)
        else:
            emit_attn_B(C, 0)
        emit_ffn(C, 0)
        if kind == "mainBf":
            emit_final(C)
        else:
            C.kv_layer = 1
            emit_kv(C, "B" if typ == "A" else "A")
            emit_store_x(C)
    C.P.final_wait("sp", C.out_toks)
    C.P.emit()
    return nc


def _slopes(n):
    return (2.0 ** (-8.0 * np.arange(1, n + 1) / n)).astype(np.float32)


def _hilo(v):
    v = np.asarray(v, np.float32)
    hi = v.astype(NPBF)
    lo = (v - hi.astype(np.float32)).astype(NPBF)
    return hi, lo


def _consts_A():
    sl = _slopes(16)
    bmat = np.zeros((128, 4, 2, 512), NPBF)
    eye = np.eye(128, dtype=np.float32)
    for g in range(4):
        for i in range(4):
            hi, lo = _hilo(sl[4 * g + i])
            bmat[:, g, 0, i * 128:(i + 1) * 128] = (eye * np.float32(hi)).astype(NPBF)
            bmat[:, g, 1, i * 128:(i + 1) * 128] = (eye * np.float32(lo)).astype(NPBF)
    dmat = np.zeros((128, 3, 128), np.float32)
    q = np.arange(128)[:, None]
    j = np.arange(128)[None, :]
    for c in range(3):
        dist = np.abs(128 * (1 - c) + q - j)
        dmat[:, c, :] = np.where(dist <= 128, -dist, -32768.0)
    oneh = np.zeros((128, 16, 16), NPBF)
    for r in range(16):
        oneh[:, r, r] = 1.0
    sel = np.zeros((16, 8, 128), NPBF)
    for g in range(4):
        for i in range(4):
            fc = 4 * (g // 2) + i
            sel[4 * g + i, fc, (g % 2) * 64:(g % 2) * 64 + 64] = 1.0
    return bmat, dmat.astype(NPBF), oneh, sel


def _consts_B():
    sl = _slopes(24).reshape(3, 2, 4)
    bmat = np.zeros((128, 6, 2, 512), NPBF)
    eye = np.eye(128, dtype=np.float32)
    for gi in range(3):
        for kv in range(2):
            for i in range(4):
                hi, lo = _hilo(sl[gi, kv, i])
                bmat[:, 2 * gi + kv, 0, i * 128:(i + 1) * 128] = (eye * np.float32(hi)).astype(NPBF)
                bmat[:, 2 * gi + kv, 1, i * 128:(i + 1) * 128] = (eye * np.float32(lo)).astype(NPBF)
    dmat = np.zeros((128, 3, 2, 128), np.float32)
    q = np.arange(128)[:, None]
    j = np.arange(128)[None, :]
    for gi, d in enumerate(B_DIL):
        for c in range(2):
            rel = np.abs(q - j + 64 - 128 * c)
            dmat[:, gi, c, :] = np.where(rel <= 64, -(d * rel), -32768.0)
    oneh = np.zeros((128, 8, 8), NPBF)
    for r in range(8):
        oneh[:, r, r] = 1.0
    sel = np.zeros((8, 4, 128), NPBF)
    for kv in range(2):
        for i in range(4):
            sel[4 * kv + i, i, kv * 64:kv * 64 + 64] = 1.0
    return bmat, dmat.astype(NPBF), oneh, sel


def _fm(v):
    v = np.asarray(v, np.float32)
    return np.ascontiguousarray(v.reshape(-1, 128).T)


_PROGS = {}


def _prog(kind):
    if kind not in _PROGS:
        _PROGS[kind] = build_mod() if kind == "mod" else build_prog(kind)
    return _PROGS[kind]


def _run(kind, in_maps):
    res = run_bass_kernel_spmd(_prog(kind), in_maps, core_ids=list(range(NCORE)))
    return res.results


def _windows(parts, H, axis):
    full = np.concatenate(parts, axis=axis)
    pad = [(0, 0)] * full.ndim
    pad[axis] = (H, H)
    full = np.pad(full, pad)
    outs = []
    for r in range(NCORE):
        idx = [slice(None)] * full.ndim
        idx[axis] = slice(r * NT, r * NT + NT + 2 * H)
        outs.append(np.ascontiguousarray(full[tuple(idx)]))
    return outs


def kernel(x, c, ada_w, ada_b, norm_mix, norm_ffn, ffn_w_in, ffn_w_out, a_w_in, a_w_out, a_sink, b_w_in, b_w_out,
           final_norm, _debug=None):
    f = lambda a: np.asarray(a, dtype=np.float32)
    x, c, ada_w, ada_b = f(x), f(c), f(ada_w), f(ada_b)
    norm_mix, norm_ffn, ffn_w_in, ffn_w_out = f(norm_mix), f(norm_ffn), f(ffn_w_in), f(ffn_w_out)
    a_w_in, a_w_out, a_sink, b_w_in, b_w_out, final_norm = (f(a_w_in), f(a_w_out), f(a_sink), f(b_w_in),
                                                            f(b_w_out), f(final_norm))
    cT = _fm(c[0])
    maps = []
    for r in range(NCORE):
        i, h = r // 2, r % 2
        maps.append({"cT": cT, "wsl": np.ascontiguousarray(ada_w[i][:, h * 3072:(h + 1) * 3072]),
                     "bsl": _fm(ada_b[i][h * 3072:(h + 1) * 3072])})
    res = _run("mod", maps)
    modT = np.concatenate([np.asarray(res[r]["modc"], np.float32) for r in range(NCORE)], axis=1)
    gainT = np.concatenate([_fm(norm_mix[i]) for i in range(4)] + [_fm(norm_ffn[i]) for i in range(4)]
                           + [_fm(final_norm)], axis=1)

    def rot(i):
        order = [(i + k) % 4 for k in range(4)]
        mt = np.concatenate([modT[:, o * 48:(o + 1) * 48] for o in order], axis=1)
        gt = np.concatenate([gainT[:, o * 8:(o + 1) * 8] for o in order]
                            + [gainT[:, 32 + o * 8:32 + (o + 1) * 8] for o in order] + [gainT[:, 64:72]], axis=1)
        return np.ascontiguousarray(mt), np.ascontiguousarray(gt)

    emasks = []
    for r in range(NCORE):
        e = np.zeros((128, 4), np.float32)
        if r == 0:
            e[:, 0] = MASKV
            e[:64, 2] = MASKV
        if r == NCORE - 1:
            e[:, 1] = MASKV
            e[64:, 3] = MASKV
        emasks.append(e)
    xT = [np.ascontiguousarray(x[0, r * NT:(r + 1) * NT, :].T) for r in range(NCORE)]

    def a_q_perm():
        cols = []
        for hc in range(8):
            for half in range(2):
                g, i = 2 * (hc // 4) + half, hc % 4
                h = 4 * g + i
                cols.append(np.arange(h * 64, h * 64 + 64))
        return np.concatenate(cols)

    def b_q_perm():
        cols = []
        for gi in range(3):
            for i in range(4):
                for kv in range(2):
                    h = kv * 4 + i
                    cols.append(gi * 512 + np.arange(h * 64, h * 64 + 64))
        return np.concatenate(cols)

    def b_o_perm():
        rows = []
        for i in range(4):
            for kv in range(2):
                h = kv * 4 + i
                rows.append(np.arange(h * 64, h * 64 + 64))
        return np.concatenate(rows)

    aqp, bqp, bop = a_q_perm(), b_q_perm(), b_o_perm()
    cA, cB = _consts_A(), _consts_B()

    def kv_weights(i):
        j = i // 2
        if i % 2 == 0:
            return (np.ascontiguousarray(a_w_in[j][:, 1024:1280]), np.ascontiguousarray(a_w_in[j][:, 1280:1536]))
        return (np.ascontiguousarray(b_w_in[j][:, 1536:1920]), np.ascontiguousarray(b_w_in[j][:, 1920:2304]))

    mt, gt = rot(0)
    wk, wv = kv_weights(0)
    maps = [{"xT": xT[r], "modT": mt, "gainT": gt, "emask": emasks[r], "wk": wk, "wv": wv} for r in range(NCORE)]
    res = _run("kv0", maps)
    KT = [np.asarray(res[r]["KT"]) for r in range(NCORE)]
    Vp = [np.asarray(res[r]["Vp"]) for r in range(NCORE)]
    dbg = {}
    for i in range(4):
        j = i // 2
        mt, gt = rot(i)
        wi_t = np.ascontiguousarray(ffn_w_in[i].reshape(KC, 128, 2, NFF, 128).transpose(1, 3, 2, 0, 4))
        wo_t = np.ascontiguousarray(ffn_w_out[i].reshape(NFF, 128, D).transpose(1, 0, 2))
        base = {"modT": mt, "gainT": gt, "w_in": wi_t, "w_out": wo_t}
        if i % 2 == 0:
            kind = "mainA"
            KTw = _windows(KT, 128, 1)
            Vw = _windows(Vp, 128, 0)
            base.update({"wq": np.ascontiguousarray(a_w_in[j][:, :1024][:, aqp]),
                         "wo": np.ascontiguousarray(a_w_out[j][aqp, :]),
                         "sink": np.ascontiguousarray(a_sink[j].reshape(16, 1)),
                         "bmat": cA[0], "dmat": cA[1], "oneh": cA[2], "sel": cA[3]})
            per = [{"KTw": KTw[r], "Vw": Vw[r]} for r in range(NCORE)]
        else:
            kind = "mainB" if i < 3 else "mainBf"
            base.update({"wq": np.ascontiguousarray(b_w_in[j][:, :1536][:, bqp]),
                         "wo": np.ascontiguousarray(b_w_out[j][bop, :]),
                         "bmat": cB[0], "dmat": cB[1], "oneh": cB[2], "sel": cB[3]})
            per = [dict() for _ in range(NCORE)]
            for gi, d in enumerate(B_DIL):
                KTw = _windows([k[gi * 128:(gi + 1) * 128] for k in KT], 64 * d, 1)
                Vw = _windows([v[:, 2 * gi:2 * gi + 2] for v in Vp], 64 * d, 0)
                for r in range(NCORE):
                    per[r][f"KTw{gi}"] = KTw[r]
                    per[r][f"Vw{gi}"] = Vw[r]
        if i < 3:
            wk, wv = kv_weights(i + 1)
            base.update({"wk": wk, "wv": wv})
        maps = []
        for r in range(NCORE):
            mp = dict(base)
            mp.update(per[r])
            mp["xT"] = xT[r]
            mp["emask"] = emasks[r]
            maps.append(mp)
        res = _run(kind, maps)
        if i < 3:
            xT = [np.asarray(res[r]["xT_out"], np.float32) for r in range(NCORE)]
            KT = [np.asarray(res[r]["KT"]) for r in range(NCORE)]
            Vp = [np.asarray(res[r]["Vp"]) for r in range(NCORE)]
            if _debug is not None:
                _debug[f"x{i}"] = np.concatenate([t.T for t in xT], axis=0)
        else:
            out = np.concatenate([np.asarray(res[r]["outT"], np.float32).T for r in range(NCORE)], axis=0)
    return np.ascontiguousarray(out.reshape(1, SEQ, D).astype(np.float32))
```

```python
import contextlib
import numpy as np
import ml_dtypes
import concourse.bass as bass
import concourse.mybir as mybir
from concourse.bass_utils import run_bass_kernel_spmd

F32 = mybir.dt.float32
BF16 = mybir.dt.bfloat16
AF = mybir.ActivationFunctionType
ALU = mybir.AluOpType
NPBF = ml_dtypes.bfloat16

NCORE = 8
NT = 2048
SEQ = 16384
D = 1024
KC = 8
TB = 512
NB = 4
DFF = 2816
NFF = 22
EPS = 1e-6
MASKV = -1.0e4
FF_GROUPS = [(0, 4), (4, 4), (8, 4), (12, 4), (16, 3), (19, 3)]
B_DIL = (1, 4, 16)
ARENA_WORDS = 52000

ENGS = ("pe", "act", "dve", "pool", "sp")


class Tok:
    __slots__ = ("name", "writer", "readers")

    def __init__(self, name):
        self.name = name
        self.writer = None
        self.readers = []


class Prog:
    def __init__(self, nc, n_dma_sems=24):
        self.nc = nc
        self.q = {e: [] for e in ENGS}
        self.cnt = {e: 0 for e in ENGS}
        self.waited = {e: {} for e in ENGS}
        self.n_dma_sems = n_dma_sems
        self.dma_cnt = [0] * n_dma_sems
        self.dma_rr = 0
        self.sems = {}
        self.ncc = 0

    def _need(self, eng, deps):
        w = self.waited[eng]
        out = {}
        for d in deps:
            if d is None:
                continue
            k, v = d
            if w.get(k, 0) >= v:
                continue
            if out.get(k, 0) < v:
                out[k] = v
        for k, v in out.items():
            w[k] = v
        return list(out.items())

    def _deps(self, reads, writes):
        deps = []
        for t in reads:
            deps.append(t.writer)
        for t in writes:
            deps.append(t.writer)
            deps.extend(t.readers)
        return deps

    def _mark(self, me, reads, writes):
        for t in reads:
            t.readers.append(me)
        for t in writes:
            t.writer = me
            t.readers = []

    def op(self, eng, fn, reads=(), writes=(), signal=True):
        deps = self._deps(reads, writes)
        if eng == "pe":
            deps = [d for d in deps if d is not None and d[0] != "pe"]
        waits = self._need(eng, deps)
        if signal:
            self.cnt[eng] += 1
            me = (eng, self.cnt[eng])
            self.q[eng].append((waits, fn, (eng, 1)))
        else:
            me = (eng, self.cnt[eng] + 1)
            self.q[eng].append((waits, fn, "nosig"))
        self._mark(me, reads, writes)
        return me

    def dma(self, eng, fn, reads=(), writes=()):
        deps = self._deps(reads, writes)
        k = self.dma_rr
        self.dma_rr = (self.dma_rr + 1) % self.n_dma_sems
        key = ("dma", k)
        if self.dma_cnt[k] > 0:
            deps.append((key, self.dma_cnt[k]))
        waits = self._need(eng, deps)
        self.dma_cnt[k] += 16
        me = (key, self.dma_cnt[k])
        self.q[eng].append((waits, fn, (key, 16)))
        self._mark(me, reads, writes)
        return me

    def coll(self, fn, reads=(), writes=()):
        deps = self._deps(reads, writes)
        waits = self._need("pool", deps)
        key = ("cc", self.ncc)
        self.ncc += 1
        me = (key, 1)
        self.q["pool"].append((waits, fn, (key, None)))
        self._mark(me, reads, writes)
        return me

    def barrier(self):
        for e in ENGS:
            deps = [(x, self.cnt[x]) for x in ENGS if x != e and self.cnt[x] > 0]
            deps += [(("dma", k), self.dma_cnt[k]) for k in range(self.n_dma_sems) if self.dma_cnt[k] > 0]
            deps += [(("cc", k), 1) for k in range(self.ncc)]
            waits = self._need(e, deps)
            if waits:
                self.q[e].append((waits, None, None))

    def final_wait(self, eng, toks):
        deps = [t.writer for t in toks]
        waits = self._need(eng, deps)
        self.q[eng].append((waits, None, None))

    def emit(self):
        nc = self.nc
        with contextlib.ExitStack() as st:
            for e in ENGS:
                self.sems[e] = st.enter_context(nc.semaphore(f"s_{e}"))
            for k in range(self.n_dma_sems):
                self.sems[("dma", k)] = st.enter_context(nc.semaphore(f"s_dma{k}"))
            for k in range(self.ncc):
                self.sems[("cc", k)] = st.enter_context(nc.semaphore(f"s_cc{k}"))
            block = st.enter_context(nc.Block())

            def run(engname):
                def body(engine):
                    for waits, fn, inc in self.q[engname]:
                        for k, v in waits:
                            engine.wait_ge(self.sems[k], v)
                        if fn is not None:
                            ins = fn(engine)
                            if inc == "nosig":
                                continue
                            if inc[1] is None:
                                ins.then_inc(self.sems[inc[0]])
                            else:
                                ins.then_inc(self.sems[inc[0]], inc[1])
                return body

            block.tensor(run("pe"))
            block.scalar(run("act"))
            block.vector(run("dve"))
            block.gpsimd(run("pool"))
            block.sync(run("sp"))


class Mem:
    def __init__(self, nc, words):
        self.arena = nc.alloc_sbuf_tensor("arena", [128, words], F32)
        self.words = words
        self.top = 0

    def mark(self):
        return self.top

    def release(self, m):
        self.top = m

    def f32(self, n):
        off = self.top
        self.top += (n + 7) // 8 * 8
        assert self.top <= self.words, f"SBUF arena overflow {self.top} > {self.words}"
        return self.arena[:, off:off + n]

    def bf16(self, n):
        w = (n + 1) // 2
        off = self.top
        self.top += (w + 7) // 8 * 8
        assert self.top <= self.words, f"SBUF arena overflow {self.top} > {self.words}"
        return self.arena[:, off:off + w].bitcast(BF16)[:, 0:n]


class Ctx:
    def __init__(self, nc):
        self.nc = nc
        self.P = Prog(nc)
        self.mem = Mem(nc, ARENA_WORDS)
        self.ps = [nc.alloc_psum_tensor(f"psb{b}", [128, 512], F32) for b in range(8)]
        self.pst = [Tok(f"ps{b}") for b in range(8)]
        self.dram = {}
        self.out_toks = []

    def din(self, name, shape, dt=F32):
        t = self.nc.dram_tensor(name, list(shape), dt, kind="ExternalInput").ap()
        self.dram[name] = t
        return t

    def dout(self, name, shape, dt=F32):
        t = self.nc.dram_tensor(name, list(shape), dt, kind="ExternalOutput").ap()
        self.dram[name] = t
        return t

    def mm(self, out, lhsT, rhs, start, stop, reads, writes):
        self.P.op("pe", lambda e: e.matmul(out, lhsT=lhsT, rhs=rhs, start=start, stop=stop), reads, writes,
                  signal=bool(stop))

    def act(self, out, in_, func, reads, writes, bias=None, scale=None):
        kw = {}
        if bias is not None:
            kw["bias"] = bias
        if scale is not None:
            kw["scale"] = scale
        self.P.op("act", lambda e: e.activation(out=out, in_=in_, func=func, **kw), reads, writes)

    def tt(self, eng, out, in0, in1, op, reads, writes):
        self.P.op(eng, lambda e: e.tensor_tensor(out=out, in0=in0, in1=in1, op=op), reads, writes)

    def ts(self, eng, out, in0, s1, s2, op0, op1, reads, writes):
        if op1 is None:
            self.P.op(eng, lambda e: e.tensor_scalar(out=out, in0=in0, scalar1=s1, scalar2=None, op0=op0), reads, writes)
        else:
            self.P.op(eng, lambda e: e.tensor_scalar(out=out, in0=in0, scalar1=s1, scalar2=s2, op0=op0, op1=op1),
                      reads, writes)

    def stt(self, out, in0, scalar, in1, op0, op1, reads, writes):
        self.P.op("dve", lambda e: e.scalar_tensor_tensor(out=out, in0=in0, scalar=scalar, in1=in1, op0=op0, op1=op1),
                  reads, writes)

    def copy(self, eng, out, in_, reads, writes):
        if eng == "act":
            self.P.op("act", lambda e: e.activation(out=out, in_=in_, func=AF.Copy), reads, writes)
        else:
            self.P.op(eng, lambda e: e.tensor_copy(out=out, in_=in_), reads, writes)

    def memset(self, eng, ap, val, writes):
        self.P.op(eng, lambda e: e.memset(ap, val), (), writes)

    def dma(self, q, out, in_, reads, writes):
        self.P.dma(q, lambda e: e.dma_start(out=out, in_=in_), reads, writes)


def v3(ap, a):
    return ap.rearrange("p (a b) -> p a b", a=a)


def v4(ap, a, b):
    return ap.rearrange("p (a b c) -> p a b c", a=a, b=b)


def setup_common(C, need_x=True):
    m = C.mem
    if need_x:
        C.xT = v3(m.f32(KC * NT), KC)
        C.tx = [[Tok(f"x{kc}_{b}") for b in range(NB)] for kc in range(KC)]
        xd = C.din("xT", [D, NT]).rearrange("(kc p) t -> p kc t", p=128)
        for kc in range(KC):
            for b in range(NB):
                C.dma("sp", C.xT[:, kc, b * TB:(b + 1) * TB], xd[:, kc, b * TB:(b + 1) * TB], [], [C.tx[kc][b]])
    C.mod = m.f32(192)
    C.gain = m.f32(72)
    C.emask = m.f32(4)
    C.gsc = m.f32(64)
    C.gfin = m.f32(8)
    C.ones = m.bf16(128)
    C.t_const = Tok("const")
    tmp = m.f32(8)
    C.dma("sp", C.mod, C.din("modT", [128, 192]), [], [C.t_const])
    C.dma("sp", C.gain, C.din("gainT", [128, 72]), [], [C.t_const])
    C.dma("sp", C.emask, C.din("emask", [128, 4]), [], [C.t_const])
    C.memset("pool", C.ones, 1.0, [C.t_const])
    C.epsb = m.f32(1)
    C.memset("pool", C.epsb, 1024.0 * EPS, [C.t_const])
    for i in range(4):
        for w in range(2):
            sc = C.mod[:, i * 48 + (3 * w + 1) * 8: i * 48 + (3 * w + 1) * 8 + 8]
            g = C.gain[:, (w * 4 + i) * 8:(w * 4 + i) * 8 + 8]
            o = C.gsc[:, (i * 2 + w) * 8:(i * 2 + w) * 8 + 8]
            C.ts("dve", tmp, sc, 1.0, 32.0, ALU.add, ALU.mult, [C.t_const], [C.t_const])
            C.tt("dve", o, tmp, g, ALU.mult, [C.t_const], [C.t_const])
    C.ts("dve", C.gfin, C.gain[:, 64:72], 32.0, None, ALU.mult, None, [C.t_const], [C.t_const])


def mod_vec(C, i, v):
    return C.mod[:, i * 48 + v * 8: i * 48 + v * 8 + 8]


def emit_norm(C, gsc, sh, dst_fn, dst_tok_fn, after_blk=None):
    m = C.mem
    sq = [v3(m.bf16(KC * TB), KC) for _ in range(2)]
    tsq = [[Tok(f"sq{i}_{kc}") for kc in range(KC)] for i in range(2)]
    rstd = [m.f32(TB) for _ in range(2)]
    trs = [Tok("rstd0"), Tok("rstd1")]
    y = [m.f32(TB) for _ in range(2)]
    ty = [Tok("y0"), Tok("y1")]
    bank, tb = C.ps[7], C.pst[7]
    for b in range(NB):
        s = sq[b % 2]
        ts_ = tsq[b % 2]
        sl = slice(b * TB, (b + 1) * TB)
        for kc in range(KC):
            C.act(s[:, kc, :], C.xT[:, kc, sl], AF.Square, [C.tx[kc][b]], [ts_[kc]])
        for kc in range(KC):
            C.mm(bank[:, :], C.ones, s[:, kc, :], kc == 0, kc == KC - 1, [ts_[kc], C.t_const], [tb])
        r = rstd[b % 2]
        C.act(r, bank[:, :], AF.Ln, [tb, C.t_const], [trs[b % 2]], bias=C.epsb[:, 0:1])
        C.act(r, r, AF.Exp, [trs[b % 2]], [trs[b % 2]], scale=-0.5)
        for kc in range(KC):
            yy = y[kc % 2]
            C.tt("dve", yy, C.xT[:, kc, sl], r, ALU.mult, [C.tx[kc][b], trs[b % 2]], [ty[kc % 2]])
            if sh is not None:
                C.act(dst_fn(kc, b), yy, AF.Identity, [ty[kc % 2], C.t_const], [dst_tok_fn(kc, b)],
                      bias=sh[:, kc:kc + 1], scale=gsc[:, kc:kc + 1])
            else:
                C.act(dst_fn(kc, b), yy, AF.Identity, [ty[kc % 2], C.t_const], [dst_tok_fn(kc, b)],
                      scale=gsc[:, kc:kc + 1])
            if after_blk is not None:
                after_blk(kc, b)


def load_w(C, dst, src, tok):
    C.dma("pool", dst, src, [], [tok])


def emit_h(C, layer, which):
    hT = v3(C.mem.bf16(KC * NT), KC)
    th = [[Tok(f"h{kc}_{b}") for b in range(NB)] for kc in range(KC)]
    gsc = C.gsc[:, (layer * 2 + which) * 8:(layer * 2 + which) * 8 + 8]
    sh = mod_vec(C, layer, 3 * which)
    emit_norm(C, gsc, sh, lambda kc, b: hT[:, kc, b * TB:(b + 1) * TB], lambda kc, b: th[kc][b])
    return hT, th


def emit_kv(C, typ):
    m = C.mem
    mk = m.mark()
    CK = 256 if typ == "A" else 384
    nch, nkv = CK // 128, CK // 64
    layer = C.kv_layer
    wk = v3(m.bf16(KC * CK), KC)
    wv = v3(m.bf16(KC * CK), KC)
    twk, twv = Tok("wk"), Tok("wv")
    wkd = C.din("wk", [D, CK]).rearrange("(kc p) n -> p kc n", p=128)
    wvd = C.din("wv", [D, CK]).rearrange("(kc p) n -> p kc n", p=128)
    load_w(C, wk, wkd, twk)
    load_w(C, wv, wvd, twv)
    kT = v3(m.bf16(nch * NT), nch)
    tk = [Tok(f"kT{c}") for c in range(nch)]
    Vp = m.bf16(16 * nkv * 128)
    tv = Tok("Vp")
    C.memset("pool", Vp, 0.0, [tv])
    Vp5 = Vp.rearrange("p (t k two d) -> p t k two d", t=16, k=nkv // 2, two=2)
    hT, th = emit_h(C, layer, 0)
    KTd = C.dout("KT", [CK, NT], BF16)
    Vpd = C.dout("Vp", [NT, nkv, 128], BF16)
    cnt = 0
    for b in range(NB):
        sl = slice(b * TB, (b + 1) * TB)
        for ch in range(nch):
            bk = 5 + (cnt % 2)
            cnt += 1
            for kc in range(KC):
                C.mm(C.ps[bk][:, :], wk[:, kc, ch * 128:(ch + 1) * 128], hT[:, kc, sl], kc == 0, kc == KC - 1,
                     [twk, th[kc][b]], [C.pst[bk]])
            C.copy("act", kT[:, ch, sl], C.ps[bk][:, :], [C.pst[bk]], [tk[ch]])
    for ch in range(nch):
        C.dma("sp", KTd[ch * 128:(ch + 1) * 128, :], kT[:, ch, :], [tk[ch]], [tk[ch]])
    for tt_ in range(16):
        b = tt_ // 4
        bk = 5 + (cnt % 2)
        cnt += 1
        for kc in range(KC):
            C.mm(C.ps[bk][:, 0:CK], hT[:, kc, tt_ * 128:(tt_ + 1) * 128], wv[:, kc, :], kc == 0, kc == KC - 1,
                 [twv, th[kc][b]], [C.pst[bk]])
        pv = C.ps[bk][:, 0:CK].rearrange("p (k two d) -> p k two d", k=nkv // 2, two=2)
        C.copy("dve", Vp5[:, tt_, :, 0, 0:64], pv[:, :, 0, :], [C.pst[bk]], [tv])
        C.copy("act", Vp5[:, tt_, :, 1, 64:128], pv[:, :, 1, :], [C.pst[bk]], [tv])
    C.dma("sp", Vpd.rearrange("(t p) k d -> p t k d", p=128), Vp.rearrange("p (t k d) -> p t k d", t=16, k=nkv),
          [tv], [tv])
    C.out_toks += tk + [tv]
    C.P.barrier()
    m.release(mk)


def emit_q(C, layer, wq_name, ncol):
    m = C.mem
    nq = ncol // 128
    qT = v3(m.bf16(nq * NT), nq)
    tq = [[Tok(f"q{c}_{b}") for b in range(NB)] for c in range(nq)]
    mk = m.mark()
    wq = v3(m.bf16(KC * ncol), KC)
    twq = Tok("wq")
    wqd = C.din(wq_name, [D, ncol]).rearrange("(kc p) n -> p kc n", p=128)
    for kc in range(KC):
        load_w(C, wq[:, kc, :], wqd[:, kc, :], twq)
    hT, th = emit_h(C, layer, 0)
    cnt = 0
    for b in range(NB):
        sl = slice(b * TB, (b + 1) * TB)
        for c in range(nq):
            bk = 5 + (cnt % 2)
            cnt += 1
            for kc in range(KC):
                C.mm(C.ps[bk][:, :], wq[:, kc, c * 128:(c + 1) * 128], hT[:, kc, sl], kc == 0, kc == KC - 1,
                     [twq, th[kc][b]], [C.pst[bk]])
            C.act(qT[:, c, sl], C.ps[bk][:, :], AF.Copy, [C.pst[bk]], [tq[c][b]], scale=0.125)
    C.P.barrier()
    m.release(mk)
    return qT, tq


def emit_finalize_outproj(C, layer, b, R, nfc, accO_fn, taccO, accL_ap, taccL, sel, wo, two, tmp):
    rl, rh, rlo, rlbc, OTn, t_rl, t_rlbc, t_otn = tmp
    sl = slice(b * TB, (b + 1) * TB)
    C.P.op("dve", lambda e: e.reciprocal(out=rl[0:R, :], in_=accL_ap), [taccL], [t_rl])
    C.copy("dve", rh[0:R, :], rl[0:R, :], [t_rl], [t_rl])
    C.tt("dve", rlo[0:R, :], rl[0:R, :], rh[0:R, :], ALU.subtract, [t_rl], [t_rl])
    for fc in range(nfc):
        bk = 5 + (fc % 2)
        C.mm(C.ps[bk][:, :], sel[0:R, fc, :], rh[0:R, :], True, False, [t_rl, C.t_const], [C.pst[bk]])
        C.mm(C.ps[bk][:, :], sel[0:R, fc, :], rlo[0:R, :], False, True, [t_rl, C.t_const], [C.pst[bk]])
        C.copy("act", rlbc[fc % 2], C.ps[bk][:, :], [C.pst[bk]], [t_rlbc[fc % 2]])
        C.tt("dve", OTn[:, fc, :], accO_fn(fc), rlbc[fc % 2], ALU.mult, [taccO, t_rlbc[fc % 2]], [t_otn[fc]])
    g1 = mod_vec(C, layer, 2)
    for mch in range(KC):
        bk = 5 + (mch % 2)
        for fc in range(nfc):
            C.mm(C.ps[bk][:, :], wo[:, fc, mch * 128:(mch + 1) * 128], OTn[:, fc, :], fc == 0, fc == nfc - 1,
                 [two, t_otn[fc]], [C.pst[bk]])
        xs = C.xT[:, mch, sl]
        C.stt(xs, C.ps[bk][:, :], g1[:, mch:mch + 1], xs, ALU.mult, ALU.add,
              [C.pst[bk], C.tx[mch][b], C.t_const], [C.tx[mch][b]])


def alloc_fin_tmp(C, nfc):
    m = C.mem
    rl = m.f32(TB)
    rh = m.bf16(TB)
    rlo = m.bf16(TB)
    rlbc = [m.f32(TB), m.f32(TB)]
    OTn = v3(m.bf16(nfc * TB), nfc)
    return (rl, rh, rlo, rlbc, OTn, Tok("rl"), [Tok("rlbc0"), Tok("rlbc1")], [Tok(f"otn{f}") for f in range(nfc)])


def emit_attn_A(C, layer):
    m = C.mem
    P = C.P
    mk0 = m.mark()
    qT, tq = emit_q(C, layer, "wq", 1024)
    bmat = v4(m.bf16(4 * 2 * 512), 4, 2)
    dmat = v3(m.bf16(3 * 128), 3)
    oneh = v3(m.bf16(16 * 16), 16)
    sel = v3(m.bf16(8 * 128), 8)
    esink = m.f32(1)
    tc = Tok("attc")
    C.dma("sp", bmat, C.din("bmat", [128, 4, 2, 512], BF16), [], [tc])
    C.dma("sp", dmat, C.din("dmat", [128, 3, 128], BF16), [], [tc])
    C.dma("sp", oneh, C.din("oneh", [128, 16, 16], BF16), [], [tc])
    C.dma("sp", sel[0:16], C.din("sel", [16, 8, 128], BF16), [], [tc])
    C.dma("sp", esink[0:16, :], C.din("sink", [16, 1]), [], [tc])
    C.act(esink[0:16, :], esink[0:16, :], AF.Exp, [tc], [tc])
    wo = v3(m.bf16(KC * D), KC)
    two = Tok("wo")
    wod = C.din("wo", [D, D]).rearrange("(kc p) n -> p kc n", p=128)
    for kc in range(KC):
        load_w(C, wo[:, kc, :], wod[:, kc, :], two)
    kTw = [v3(m.bf16(2 * 768), 2) for _ in range(2)]
    Vw = [v4(m.bf16(6 * 4 * 128), 6, 4) for _ in range(2)]
    tkw = [Tok("kTw0"), Tok("kTw1")]
    tvw = [Tok("Vw0"), Tok("Vw1")]
    accO = v3(m.f32(8 * TB), 8)
    taccO = Tok("accO")
    accL = m.f32(TB)
    taccL = Tok("accL")
    PT = [m.bf16(512), m.bf16(512)]
    tpt = [Tok("pt0"), Tok("pt1")]
    fin = alloc_fin_tmp(C, 8)
    KTd = C.din("KTw", [256, NT + 256], BF16)
    Vd = C.din("Vw", [NT + 256, 4, 128], BF16).rearrange("(t p) k d -> p t k d", p=128)
    scnt = 0
    pend = []

    def flush():
        while pend:
            pend.pop(0)()
    for b in range(NB):
        kb, vb = kTw[b % 2], Vw[b % 2]
        for ch in range(2):
            C.dma("sp", kb[:, ch, :], KTd[ch * 128:(ch + 1) * 128, b * TB:b * TB + 768], [], [tkw[b % 2]])
        C.dma("sp", vb, Vd[:, 4 * b:4 * b + 6], [], [tvw[b % 2]])
        for qi in range(4):
            j = 4 * b + qi
            qs = slice(b * TB + qi * 128, b * TB + (qi + 1) * 128)
            lbank, tl = C.ps[4], C.pst[4]
            first_l = True
            for pair in range(2):
                obank, to = C.ps[2 + pair], C.pst[2 + pair]
                first_o = True
                for g in (2 * pair, 2 * pair + 1):
                    half = g % 2
                    hs = slice(half * 64, half * 64 + 64)
                    for c in range(3):
                        sb = scnt % 2
                        scnt += 1
                        sbank, tsb = C.ps[sb], C.pst[sb]
                        C.mm(sbank[:, :], dmat[:, c, :], bmat[:, g, 0, :], True, False, [tc], [tsb])
                        C.mm(sbank[:, :], dmat[:, c, :], bmat[:, g, 1, :], False, False, [tc], [tsb])
                        C.mm(sbank[:, :], kb[hs, pair, (qi + c) * 128:(qi + c + 1) * 128],
                             qT[hs, 4 * pair:4 * pair + 4, qs], False, True,
                             [tkw[b % 2]] + [tq[4 * pair + i][b] for i in range(4)], [tsb])
                        bias = None
                        if j == 0 and c == 0:
                            bias = C.emask[:, 0:1]
                        if j == 15 and c == 2:
                            bias = C.emask[:, 1:2]
                        pt, tp = PT[sb], tpt[sb]
                        C.act(pt, sbank[:, :], AF.Exp, [tsb, C.t_const], [tp], bias=bias)
                        last_o = (g == 2 * pair + 1 and c == 2)
                        flush()

                        def pv(pt=pt, tp=tp, obank=obank, to=to, lbank=lbank, tl=tl, g=g, c=c, qi=qi, pair=pair,
                               first_o=first_o, last_o=last_o, first_l=first_l, vb=vb, b=b):
                            C.mm(obank[:, :], vb[:, qi + c, g, :], pt, first_o, last_o, [tp, tvw[b % 2]], [to])
                            for i in range(4):
                                last_l = (pair == 1 and last_o and i == 3)
                                C.mm(lbank[0:16, 0:128], oneh[:, 4 * g + i, :], pt[:, i * 128:(i + 1) * 128],
                                     first_l and i == 0, last_l, [tp, tc], [tl])
                            if last_o:
                                C.copy("act", accO[:, 4 * pair:4 * pair + 4, qi * 128:(qi + 1) * 128],
                                       obank[:, :].rearrange("p (i q) -> p i q", i=4), [to], [taccO])
                                if pair == 1:
                                    C.ts("dve", accL[0:16, qi * 128:(qi + 1) * 128], lbank[0:16, 0:128],
                                         esink[0:16, 0:1], None, ALU.add, None, [tl, tc], [taccL])
                        pend.append(pv)
                        first_o = False
                        first_l = False
        flush()
        emit_finalize_outproj(C, layer, b, 16, 8, lambda fc: accO[:, fc, :], taccO, accL[0:16, :], taccL,
                              sel, wo, two, fin)
    P.barrier()
    m.release(mk0)


def emit_attn_B(C, layer):
    m = C.mem
    P = C.P
    mk0 = m.mark()
    qT, tq = emit_q(C, layer, "wq", 1536)
    accO = v3(m.f32(4 * NT), 4)
    accL = m.f32(NT)
    taccO, taccL = Tok("accO"), Tok("accL")
    C.memset("pool", accO.rearrange("p a b -> p (a b)"), 0.0, [taccO])
    C.memset("pool", accL, 0.0, [taccL])
    oneh = v3(m.bf16(8 * 8), 8)
    sel = v3(m.bf16(4 * 128), 4)
    tc = Tok("attc")
    C.dma("sp", oneh, C.din("oneh", [128, 8, 8], BF16), [], [tc])
    C.dma("sp", sel[0:8], C.din("sel", [8, 4, 128], BF16), [], [tc])
    mk1 = m.mark()
    bmat = v4(m.bf16(6 * 2 * 512), 6, 2)
    dmat = v4(m.bf16(3 * 2 * 128), 3, 2)
    C.dma("sp", bmat, C.din("bmat", [128, 6, 2, 512], BF16), [], [tc])
    C.dma("sp", dmat, C.din("dmat", [128, 3, 2, 128], BF16), [], [tc])
    PT = [m.bf16(512), m.bf16(512)]
    tpt = [Tok("pt0"), Tok("pt1")]
    WMAX = NT + 128 * 16
    kTw_buf = m.bf16(WMAX)
    Vw_buf = m.bf16(32 * 2 * 128)
    tkw, tvw = Tok("kTw"), Tok("Vw")
    scnt = 0
    ocnt = 0
    pendB = []

    def flushB():
        while pendB:
            pendB.pop(0)()

    for gi, d in enumerate(B_DIL):
        W = NT + 128 * d
        ncw = 16 // d + 1
        U = NT // d
        nut = U // 128
        kTw = kTw_buf[:, 0:W]
        Vw = v4(Vw_buf[:, 0:d * ncw * 256], d * ncw, 2)
        C.dma("sp", kTw, C.din(f"KTw{gi}", [128, W], BF16), [], [tkw])
        Vd = C.din(f"Vw{gi}", [W, 2, 128], BF16).rearrange("(w dd) k e -> dd w k e", dd=d)
        for rho in range(d):
            C.dma("sp", Vw[:, rho * ncw:(rho + 1) * ncw], Vd[rho].rearrange("(cw p) k e -> p cw k e", p=128),
                  [], [tvw])
        for rho in range(d):
            for ut in range(nut):
                t0 = rho + d * 128 * ut
                qsl = slice(t0, t0 + 127 * d + 1, d)
                blks = sorted(set([t0 // TB, (t0 + d * 127) // TB]))
                blks = list(range(blks[0], blks[-1] + 1))
                ob = 2 + (ocnt % 2)
                lb = 4 + (ocnt % 2)
                ocnt += 1
                obank, to = C.ps[ob], C.pst[ob]
                lbank, tl = C.ps[lb], C.pst[lb]
                first_o, first_l = True, True
                for kv in range(2):
                    hs = slice(kv * 64, kv * 64 + 64)
                    for c in range(2):
                        sb = scnt % 2
                        scnt += 1
                        sbank, tsb = C.ps[sb], C.pst[sb]
                        k0 = rho + d * 128 * (ut + c)
                        C.mm(sbank[:, :], dmat[:, gi, c, :], bmat[:, 2 * gi + kv, 0, :], True, False, [tc], [tsb])
                        C.mm(sbank[:, :], dmat[:, gi, c, :], bmat[:, 2 * gi + kv, 1, :], False, False, [tc], [tsb])
                        C.mm(sbank[:, :], kTw[hs, k0:k0 + 127 * d + 1:d], qT[hs, 4 * gi:4 * gi + 4, qsl], False, True,
                             [tkw] + [tq[4 * gi + i][bb] for i in range(4) for bb in blks], [tsb])
                        bias = None
                        if ut == 0 and c == 0:
                            bias = C.emask[:, 2:3]
                        if ut == nut - 1 and c == 1:
                            bias = C.emask[:, 3:4]
                        pt, tp = PT[sb], tpt[sb]
                        C.act(pt, sbank[:, :], AF.Exp, [tsb, C.t_const], [tp], bias=bias)
                        last = (kv == 1 and c == 1)
                        flushB()

                        def pv(pt=pt, tp=tp, obank=obank, to=to, lbank=lbank, tl=tl, kv=kv, c=c, qsl=qsl,
                               first_o=first_o, first_l=first_l, last=last, vidx=rho * ncw + ut + c, Vw=Vw):
                            C.mm(obank[:, :], Vw[:, vidx, kv, :], pt, first_o, last, [tp, tvw], [to])
                            for i in range(4):
                                C.mm(lbank[0:8, 0:128], oneh[:, 4 * kv + i, :], pt[:, i * 128:(i + 1) * 128],
                                     first_l and i == 0, last and i == 3, [tp, tc], [tl])
                            if last:
                                av = accO[:, :, qsl]
                                C.tt("dve", av, obank[:, :].rearrange("p (i q) -> p i q", i=4), av, ALU.add,
                                     [to, taccO], [taccO])
                                lv = accL[0:8, qsl]
                                C.tt("dve", lv, lbank[0:8, 0:128], lv, ALU.add, [tl, taccL], [taccL])
                        pendB.append(pv)
                        first_o = False
                        first_l = False
        flushB()
    P.barrier()
    m.release(mk1)
    wo = v3(m.bf16(4 * D), 4)
    two = Tok("wo")
    wod = C.din("wo", [512, D]).rearrange("(kc p) n -> p kc n", p=128)
    load_w(C, wo, wod, two)
    fin = alloc_fin_tmp(C, 4)
    for b in range(NB):
        sl = slice(b * TB, (b + 1) * TB)
        emit_finalize_outproj(C, layer, b, 8, 4, lambda fc, sl=sl: accO[:, fc, sl], taccO, accL[0:8, sl], taccL,
                              sel, wo, two, fin)
    P.barrier()
    m.release(mk0)


def emit_ffn(C, layer):
    m = C.mem
    P = C.P
    mk0 = m.mark()
    hT, th = emit_h(C, layer, 1)
    wgu = [m.bf16(4 * 2 * KC * 128).rearrange("p (c s k j) -> p c s k j", c=4, s=2, k=KC) for _ in range(2)]
    wo = [v3(m.bf16(4 * D), 4) for _ in range(2)]
    tw = [Tok("ffw0"), Tok("ffw1")]
    actb = [v3(m.bf16(4 * TB), 4) for _ in range(2)]
    tact = [[Tok(f"act{i}_{c}") for c in range(4)] for i in range(2)]
    sg = [m.f32(TB), m.f32(TB)]
    tsg = [Tok("sg0"), Tok("sg1")]
    wind = C.din("w_in", [128, NFF, 2, KC, 128])
    woutd = C.din("w_out", [128, NFF, D])
    g2 = mod_vec(C, layer, 5)

    def load_group(gidx):
        c0, G = FF_GROUPS[gidx]
        i = gidx % 2
        load_w(C, wgu[i][:, 0:G], wind[:, c0:c0 + G], tw[i])
        load_w(C, wo[i][:, 0:G, :], woutd[:, c0:c0 + G, :], tw[i])

    load_group(0)
    cnt = 0
    ab = 0
    pendF = []

    def flushF():
        while pendF:
            pendF.pop(0)()

    for gidx, (c0, G) in enumerate(FF_GROUPS):
        i = gidx % 2
        for b in range(NB):
            sl = slice(b * TB, (b + 1) * TB)
            a = actb[ab % 2]
            ta = tact[ab % 2]
            ab += 1
            for c in range(G):
                gb, ub = 0 + (cnt % 2), 2 + (cnt % 2)
                s = cnt % 2
                cnt += 1
                for kc in range(KC):
                    C.mm(C.ps[gb][:, :], wgu[i][:, c, 0, kc, :], hT[:, kc, sl], kc == 0, kc == KC - 1,
                         [tw[i], th[kc][b]], [C.pst[gb]])
                for kc in range(KC):
                    C.mm(C.ps[ub][:, :], wgu[i][:, c, 1, kc, :], hT[:, kc, sl], kc == 0, kc == KC - 1,
                         [tw[i], th[kc][b]], [C.pst[ub]])
                C.act(sg[s], C.ps[gb][:, :], AF.Silu, [C.pst[gb]], [tsg[s]])
                C.tt("dve", a[:, c, :], sg[s], C.ps[ub][:, :], ALU.mult, [tsg[s], C.pst[ub]], [ta[c]])
            flushF()
            if b == 0 and gidx + 1 < len(FF_GROUPS):
                load_group(gidx + 1)

            def outproj(i=i, G=G, a=a, ta=ta, sl=sl, b=b):
                for mch in range(KC):
                    yb = 4 + (mch % 4)
                    for c in range(G):
                        C.mm(C.ps[yb][:, :], wo[i][:, c, mch * 128:(mch + 1) * 128], a[:, c, :], c == 0, c == G - 1,
                             [tw[i], ta[c]], [C.pst[yb]])
                    xs = C.xT[:, mch, sl]
                    C.stt(xs, C.ps[yb][:, :], g2[:, mch:mch + 1], xs, ALU.mult, ALU.add,
                          [C.pst[yb], C.tx[mch][b], C.t_const], [C.tx[mch][b]])
            pendF.append(outproj)
    flushF()
    P.barrier()
    m.release(mk0)


def emit_store_x(C):
    xd = C.dout("xT_out", [D, NT]).rearrange("(kc p) t -> p kc t", p=128)
    for kc in range(KC):
        for b in range(NB):
            C.dma("sp", xd[:, kc, b * TB:(b + 1) * TB], C.xT[:, kc, b * TB:(b + 1) * TB], [C.tx[kc][b]], [C.tx[kc][b]])
            C.out_toks.append(C.tx[kc][b])


def emit_final(C):
    m = C.mem
    od = C.dout("outT", [D, NT]).rearrange("(kc p) t -> p kc t", p=128)
    obuf = [m.f32(TB) for _ in range(4)]
    tob = [Tok(f"ob{i}") for i in range(4)]
    st = {"n": 0}

    def dst(kc, b):
        return obuf[st["n"] % 4]

    def dtok(kc, b):
        return tob[st["n"] % 4]

    def after(kc, b):
        i = st["n"] % 4
        C.dma("sp", od[:, kc, b * TB:(b + 1) * TB], obuf[i], [tob[i]], [tob[i]])
        st["n"] += 1

    emit_norm(C, C.gfin, None, dst, dtok, after_blk=after)
    C.out_toks += tob


def build_mod():
    nc = bass.Bass("TRN2", target_bir_lowering=False)
    C = Ctx(nc)
    m = C.mem
    cT = m.f32(8)
    cb = m.bf16(8)
    bsl = m.f32(24)
    res = m.f32(24)
    w = v3(m.bf16(KC * 3072), KC)
    tcn, tw, tr = Tok("c"), Tok("w"), Tok("res")
    C.dma("sp", cT, C.din("cT", [128, 8]), [], [tcn])
    C.dma("sp", bsl, C.din("bsl", [128, 24]), [], [tcn])
    wd = C.din("wsl", [D, 3072]).rearrange("(kc p) n -> p kc n", p=128)
    for kc in range(KC):
        load_w(C, w[:, kc, :], wd[:, kc, :], tw)
    C.act(cb, cT, AF.Silu, [tcn], [tcn])
    for j in range(24):
        for kc in range(KC):
            C.mm(C.ps[0][:, j:j + 1], w[:, kc, j * 128:(j + 1) * 128], cb[:, kc:kc + 1], kc == 0, kc == KC - 1,
                 [tw, tcn], [C.pst[0]])
    C.tt("dve", res, C.ps[0][:, 0:24], bsl, ALU.add, [C.pst[0], tcn], [tr])
    C.dma("sp", C.dout("modc", [128, 24]), res, [tr], [tr])
    C.P.final_wait("sp", [tr])
    C.P.emit()
    return nc


def build_prog(kind):
    nc = bass.Bass("TRN2", target_bir_lowering=False)
    C = Ctx(nc)
    setup_common(C)
    if kind == "kv0":
        C.kv_layer = 0
        emit_kv(C, "A")
    else:
        typ = "A" if kind == "mainA" else "B"
        if typ == "A":
            emit_attn_A(C, 0)
        else:
            emit_attn_B(C, 0)
        emit_ffn(C, 0)
        if kind == "mainBf":
            emit_final(C)
        else:
            C.kv_layer = 1
            emit_kv(C, "B" if typ == "A" else "A")
            emit_store_x(C)
    C.P.final_wait("sp", C.out_toks)
    C.P.emit()
    return nc


def _slopes(n):
    return (2.0 ** (-8.0 * np.arange(1, n + 1) / n)).astype(np.float32)


def _hilo(v):
    v = np.asarray(v, np.float32)
    hi = v.astype(NPBF)
    lo = (v - hi.astype(np.float32)).astype(NPBF)
    return hi, lo


def _consts_A():
    sl = _slopes(16)
    bmat = np.zeros((128, 4, 2, 512), NPBF)
    eye = np.eye(128, dtype=np.float32)
    for g in range(4):
        for i in range(4):
            hi, lo = _hilo(sl[4 * g + i])
            bmat[:, g, 0, i * 128:(i + 1) * 128] = (eye * np.float32(hi)).astype(NPBF)
            bmat[:, g, 1, i * 128:(i + 1) * 128] = (eye * np.float32(lo)).astype(NPBF)
    dmat = np.zeros((128, 3, 128), np.float32)
    q = np.arange(128)[:, None]
    j = np.arange(128)[None, :]
    for c in range(3):
        dist = np.abs(128 * (1 - c) + q - j)
        dmat[:, c, :] = np.where(dist <= 128, -dist, -32768.0)
    oneh = np.zeros((128, 16, 16), NPBF)
    for r in range(16):
        oneh[:, r, r] = 1.0
    sel = np.zeros((16, 8, 128), NPBF)
    for g in range(4):
        for i in range(4):
            fc = 4 * (g // 2) + i
            sel[4 * g + i, fc, (g % 2) * 64:(g % 2) * 64 + 64] = 1.0
    return bmat, dmat.astype(NPBF), oneh, sel


def _consts_B():
    sl = _slopes(24).reshape(3, 2, 4)
    bmat = np.zeros((128, 6, 2, 512), NPBF)
    eye = np.eye(128, dtype=np.float32)
    for gi in range(3):
        for kv in range(2):
            for i in range(4):
                hi, lo = _hilo(sl[gi, kv, i])
                bmat[:, 2 * gi + kv, 0, i * 128:(i + 1) * 128] = (eye * np.float32(hi)).astype(NPBF)
                bmat[:, 2 * gi + kv, 1, i * 128:(i + 1) * 128] = (eye * np.float32(lo)).astype(NPBF)
    dmat = np.zeros((128, 3, 2, 128), np.float32)
    q = np.arange(128)[:, None]
    j = np.arange(128)[None, :]
    for gi, d in enumerate(B_DIL):
        for c in range(2):
            rel = np.abs(q - j + 64 - 128 * c)
            dmat[:, gi, c, :] = np.where(rel <= 64, -(d * rel), -32768.0)
    oneh = np.zeros((128, 8, 8), NPBF)
    for r in range(8):
        oneh[:, r, r] = 1.0
    sel = np.zeros((8, 4, 128), NPBF)
    for kv in range(2):
        for i in range(4):
            sel[4 * kv + i, i, kv * 64:kv * 64 + 64] = 1.0
    return bmat, dmat.astype(NPBF), oneh, sel


def _fm(v):
    v = np.asarray(v, np.float32)
    return np.ascontiguousarray(v.reshape(-1, 128).T)


_PROGS = {}


def _prog(kind):
    if kind not in _PROGS:
        _PROGS[kind] = build_mod() if kind == "mod" else build_prog(kind)
    return _PROGS[kind]


def _run(kind, in_maps):
    res = run_bass_kernel_spmd(_prog(kind), in_maps, core_ids=list(range(NCORE)))
    return res.results


def _windows(parts, H, axis):
    full = np.concatenate(parts, axis=axis)
    pad = [(0, 0)] * full.ndim
    pad[axis] = (H, H)
    full = np.pad(full, pad)
    outs = []
    for r in range(NCORE):
        idx = [slice(None)] * full.ndim
        idx[axis] = slice(r * NT, r * NT + NT + 2 * H)
        outs.append(np.ascontiguousarray(full[tuple(idx)]))
    return outs


def kernel(x, c, ada_w, ada_b, norm_mix, norm_ffn, ffn_w_in, ffn_w_out, a_w_in, a_w_out, a_sink, b_w_in, b_w_out,
           final_norm, _debug=None):
    f = lambda a: np.asarray(a, dtype=np.float32)
    x, c, ada_w, ada_b = f(x), f(c), f(ada_w), f(ada_b)
    norm_mix, norm_ffn, ffn_w_in, ffn_w_out = f(norm_mix), f(norm_ffn), f(ffn_w_in), f(ffn_w_out)
    a_w_in, a_w_out, a_sink, b_w_in, b_w_out, final_norm = (f(a_w_in), f(a_w_out), f(a_sink), f(b_w_in),
                                                            f(b_w_out), f(final_norm))
    cT = _fm(c[0])
    maps = []
    for r in range(NCORE):
        i, h = r // 2, r % 2
        maps.append({"cT": cT, "wsl": np.ascontiguousarray(ada_w[i][:, h * 3072:(h + 1) * 3072]),
                     "bsl": _fm(ada_b[i][h * 3072:(h + 1) * 3072])})
    res = _run("mod", maps)
    modT = np.concatenate([np.asarray(res[r]["modc"], np.float32) for r in range(NCORE)], axis=1)
    gainT = np.concatenate([_fm(norm_mix[i]) for i in range(4)] + [_fm(norm_ffn[i]) for i in range(4)]
                           + [_fm(final_norm)], axis=1)

    def rot(i):
        order = [(i + k) % 4 for k in range(4)]
        mt = np.concatenate([modT[:, o * 48:(o + 1) * 48] for o in order], axis=1)
        gt = np.concatenate([gainT[:, o * 8:(o + 1) * 8] for o in order]
                            + [gainT[:, 32 + o * 8:32 + (o + 1) * 8] for o in order] + [gainT[:, 64:72]], axis=1)
        return np.ascontiguousarray(mt), np.ascontiguousarray(gt)

    emasks = []
    for r in range(NCORE):
        e = np.zeros((128, 4), np.float32)
        if r == 0:
            e[:, 0] = MASKV
            e[:64, 2] = MASKV
        if r == NCORE - 1:
            e[:, 1] = MASKV
            e[64:, 3] = MASKV
        emasks.append(e)
    xT = [np.ascontiguousarray(x[0, r * NT:(r + 1) * NT, :].T) for r in range(NCORE)]

    def a_q_perm():
        cols = []
        for hc in range(8):
            for half in range(2):
                g, i = 2 * (hc // 4) + half, hc % 4
                h = 4 * g + i
                cols.append(np.arange(h * 64, h * 64 + 64))
        return np.concatenate(cols)

    def b_q_perm():
        cols = []
        for gi in range(3):
            for i in range(4):
                for kv in range(2):
                    h = kv * 4 + i
                    cols.append(gi * 512 + np.arange(h * 64, h * 64 + 64))
        return np.concatenate(cols)

    def b_o_perm():
        rows = []
        for i in range(4):
            for kv in range(2):
                h = kv * 4 + i
                rows.append(np.arange(h * 64, h * 64 + 64))
        return np.concatenate(rows)

    aqp, bqp, bop = a_q_perm(), b_q_perm(), b_o_perm()
    cA, cB = _consts_A(), _consts_B()

    def kv_weights(i):
        j = i // 2
        if i % 2 == 0:
            return (np.ascontiguousarray(a_w_in[j][:, 1024:1280]), np.ascontiguousarray(a_w_in[j][:, 1280:1536]))
        return (np.ascontiguousarray(b_w_in[j][:, 1536:1920]), np.ascontiguousarray(b_w_in[j][:, 1920:2304]))

    mt, gt = rot(0)
    wk, wv = kv_weights(0)
    maps = [{"xT": xT[r], "modT": mt, "gainT": gt, "emask": emasks[r], "wk": wk, "wv": wv} for r in range(NCORE)]
    res = _run("kv0", maps)
    KT = [np.asarray(res[r]["KT"]) for r in range(NCORE)]
    Vp = [np.asarray(res[r]["Vp"]) for r in range(NCORE)]
    dbg = {}
    for i in range(4):
        j = i // 2
        mt, gt = rot(i)
        wi_t = np.ascontiguousarray(ffn_w_in[i].reshape(KC, 128, 2, NFF, 128).transpose(1, 3, 2, 0, 4))
        wo_t = np.ascontiguousarray(ffn_w_out[i].reshape(NFF, 128, D).transpose(1, 0, 2))
        base = {"modT": mt, "gainT": gt, "w_in": wi_t, "w_out": wo_t}
        if i % 2 == 0:
            kind = "mainA"
            KTw = _windows(KT, 128, 1)
            Vw = _windows(Vp, 128, 0)
            base.update({"wq": np.ascontiguousarray(a_w_in[j][:, :1024][:, aqp]),
                         "wo": np.ascontiguousarray(a_w_out[j][aqp, :]),
                         "sink": np.ascontiguousarray(a_sink[j].reshape(16, 1)),
                         "bmat": cA[0], "dmat": cA[1], "oneh": cA[2], "sel": cA[3]})
            per = [{"KTw": KTw[r], "Vw": Vw[r]} for r in range(NCORE)]
        else:
            kind = "mainB" if i < 3 else "mainBf"
            base.update({"wq": np.ascontiguousarray(b_w_in[j][:, :1536][:, bqp]),
                         "wo": np.ascontiguousarray(b_w_out[j][bop, :]),
                         "bmat": cB[0], "dmat": cB[1], "oneh": cB[2], "sel": cB[3]})
            per = [dict() for _ in range(NCORE)]
            for gi, d in enumerate(B_DIL):
                KTw = _windows([k[gi * 128:(gi + 1) * 128] for k in KT], 64 * d, 1)
                Vw = _windows([v[:, 2 * gi:2 * gi + 2] for v in Vp], 64 * d, 0)
                for r in range(NCORE):
                    per[r][f"KTw{gi}"] = KTw[r]
                    per[r][f"Vw{gi}"] = Vw[r]
        if i < 3:
            wk, wv = kv_weights(i + 1)
            base.update({"wk": wk, "wv": wv})
        maps = []
        for r in range(NCORE):
            mp = dict(base)
            mp.update(per[r])
            mp["xT"] = xT[r]
            mp["emask"] = emasks[r]
            maps.append(mp)
        res = _run(kind, maps)
        if i < 3:
            xT = [np.asarray(res[r]["xT_out"], np.float32) for r in range(NCORE)]
            KT = [np.asarray(res[r]["KT"]) for r in range(NCORE)]
            Vp = [np.asarray(res[r]["Vp"]) for r in range(NCORE)]
            if _debug is not None:
                _debug[f"x{i}"] = np.concatenate([t.T for t in xT], axis=0)
        else:
            out = np.concatenate([np.asarray(res[r]["outT"], np.float32).T for r in range(NCORE)], axis=0)
    return np.ascontiguousarray(out.reshape(1, SEQ, D).astype(np.float32))
```
